# Optimizing a Trainium2 kernel written in Bass

```python
import jax
import jax.numpy as jnp
from jax import lax
import numpy as np

D_MODEL = 2048
BATCH = 16
SEQ = 256
DEPTH = 4
DEC_BATCH = 8
DEC_SEQ = 2048
PAST_LEN = 512

GRID_W = 64
N_MIXERS = 3
MIXER_OF_LAYER = tuple(i % N_MIXERS for i in range(DEPTH))
LAYER_SLOT = tuple(MIXER_OF_LAYER[:i].count(MIXER_OF_LAYER[i]) for i in range(DEPTH))
N_POOL_LAYERS = MIXER_OF_LAYER.count(0)
N_RWKV_LAYERS = MIXER_OF_LAYER.count(1)
N_ATTN_LAYERS = MIXER_OF_LAYER.count(2)

POOL_WINDOWS = (2, 4, 8, 16)
POOL_GROUP = D_MODEL // len(POOL_WINDOWS)

RWKV_HEAD = 64
RWKV_HEADS = D_MODEL // RWKV_HEAD
DECAY_LORA = 96
AAA_LORA = 96
GATE_LORA = 256
DECAY_SCALE = 0.606531
GN_EPS = 64e-5

ATTN_HEAD_DIM = 128
ATTN_HEADS = 16
ATTN_KV_HEADS = 4
ATTN_GROUP = ATTN_HEADS // ATTN_KV_HEADS
WINDOW = 128
BLOCK = 128
ROPE_BASE = 10000.0
ATTN_SCALE = ATTN_HEAD_DIM ** -0.5
NEG_INF = -1e30

D_FF = 5632
CONV_WIDTH = 3
NORM_EPS = 1e-6

kernel_name = 'hybrid_pool_rwkv7_swa_dit_step'


def rmsnorm(x, g):
    x32 = x.astype(jnp.float32)
    y = x32 * lax.rsqrt(jnp.mean(x32 * x32, axis=-1, keepdims=True) + NORM_EPS)
    return (y * g.astype(jnp.float32)).astype(x.dtype)


def adaln(cond, w, b):
    m = jax.nn.silu(cond) @ w + b
    return jnp.split(m[:, None, :], 6, axis=-1)


def modulate(x, shift, scale):
    return x * (1.0 + scale) + shift


def centred_shift(x):
    xp = jnp.pad(x, ((0, 0), (1, 1), (0, 0)))
    return 0.5 * (xp[:, :-2] + xp[:, 2:]) - x


def pool_mixer(x, w_grp, scale):
    B, T, D = x.shape
    x32 = x.astype(jnp.float32)
    cs = jnp.concatenate([jnp.zeros((B, 1, D), jnp.float32), jnp.cumsum(x32, axis=1)], axis=1)
    t = jnp.arange(T)
    outs = []
    for gi, win in enumerate(POOL_WINDOWS):
        left = win // 2
        right = win - 1 - left
        lo = jnp.clip(t - left, 0, T)
        hi = jnp.clip(t + right + 1, 0, T)
        csg = cs[:, :, gi * POOL_GROUP:(gi + 1) * POOL_GROUP]
        s = jnp.take(csg, hi, axis=1) - jnp.take(csg, lo, axis=1)
        cnt = (hi - lo).astype(jnp.float32)[None, :, None]
        outs.append(s / cnt - x32[:, :, gi * POOL_GROUP:(gi + 1) * POOL_GROUP])
    pooled = jnp.stack(outs, axis=2).astype(x.dtype)
    y = jnp.einsum('btgi,gio->btgo', pooled, w_grp).reshape(B, T, D)
    return y * scale


def rwkv_scan(s0, r, w, k, v, aa, bb, reverse):
    def step(s, inp):
        r_t, w_t, k_t, v_t, a_t, b_t = inp
        sa = jnp.einsum('bhvk,bhk->bhv', s, a_t)
        s = s * w_t[:, :, None, :] + sa[..., None] * b_t[:, :, None, :] + v_t[..., None] * k_t[:, :, None, :]
        return s, jnp.einsum('bhvk,bhk->bhv', s, r_t)
    xs = tuple(jnp.moveaxis(z, 1, 0) for z in (r, w, k, v, aa, bb))
    s_final, o = lax.scan(step, s0.astype(jnp.float32), xs, reverse=reverse)
    return s_final, jnp.moveaxis(o, 0, 1)


def rwkv_mixer(h, p, s0_fwd, s0_bwd):
    B, T, D = h.shape
    H, N = RWKV_HEADS, RWKV_HEAD
    f32 = jnp.float32
    xx = centred_shift(h)
    mu = p['mu']
    xr, xw, xk, xv, xa, xg = (h + xx * mu[i] for i in range(6))
    r = (xr @ p['w_r']).reshape(B, T, H, N).astype(f32)
    k = (xk @ p['w_k']).reshape(B, T, H, N).astype(f32)
    v = (xv @ p['w_v']).reshape(B, T, H, N).astype(f32)
    g = jax.nn.sigmoid(xg @ p['g1']) @ p['g2']
    w_lora = jnp.einsum('ebtr,erd->ebtd', jnp.tanh(jnp.einsum('btd,edr->ebtr', xw, p['w1'])), p['w2'])
    decay = jnp.exp(-DECAY_SCALE * jax.nn.sigmoid((p['w0'][:, None, None, :] + w_lora).astype(f32)))
    decay = decay.reshape(2, B, T, H, N)
    a_lora = jnp.einsum('ebtr,erd->ebtd', jnp.einsum('btd,edr->ebtr', xa, p['a1']), p['a2'])
    a = jax.nn.sigmoid((p['a0'][:, None, None, :] + a_lora).astype(f32)).reshape(2, B, T, H, N)
    kk = k * p['k_k'].reshape(H, N).astype(f32)
    kk = kk / jnp.maximum(jnp.sqrt(jnp.sum(kk * kk, axis=-1, keepdims=True)), 1e-12)
    k_dir = k[None] * (1.0 + (a - 1.0) * p['k_a'].reshape(H, N).astype(f32))
    s_fwd, o_fwd = rwkv_scan(s0_fwd, r, decay[0], k_dir[0], v, -kk, kk * a[0], reverse=False)
    s_bwd, o_bwd = rwkv_scan(s0_bwd, r, decay[1], k_dir[1], v, -kk, kk * a[1], reverse=True)
    o = o_fwd + o_bwd
    mean = jnp.mean(o, axis=-1, keepdims=True)
    var = jnp.mean(jnp.square(o - mean), axis=-1, keepdims=True)
    o = ((o - mean) * lax.rsqrt(var + GN_EPS)).reshape(B, T, D) * p['ln_w'].astype(f32) + p['ln_b'].astype(f32)
    bonus = jnp.sum(r[None] * k_dir * p['r_k'].astype(f32), axis=(0, -1))[..., None] * v
    y = ((o + bonus.reshape(B, T, D)) * g.astype(f32)).astype(h.dtype)
    return y @ p['w_o'], s_fwd, s_bwd


def attn_qkv(h, p):
    B, T, _ = h.shape
    H, KV, Dh = ATTN_HEADS, ATTN_KV_HEADS, ATTN_HEAD_DIM
    qkv = h @ p['w_qkv']
    q = qkv[..., :H * Dh].reshape(B, T, H, Dh)
    k = qkv[..., H * Dh:(H + KV) * Dh].reshape(B, T, KV, Dh)
    v = qkv[..., (H + KV) * Dh:].reshape(B, T, KV, Dh)
    return rmsnorm(q, p['q_norm']), rmsnorm(k, p['k_norm']), v


def axial_rope_tables(T):
    rows = T // GRID_W
    row = jnp.broadcast_to(jnp.arange(rows)[:, None], (rows, GRID_W)).reshape(T).astype(jnp.float32)
    col = jnp.broadcast_to(jnp.arange(GRID_W)[None, :], (rows, GRID_W)).reshape(T).astype(jnp.float32)
    n_freq = ATTN_HEAD_DIM // 4
    inv = ROPE_BASE ** (-jnp.arange(n_freq, dtype=jnp.float32) / n_freq)
    ang = jnp.stack([row[:, None] * inv, col[:, None] * inv], axis=1)
    return jnp.cos(ang), jnp.sin(ang)


def apply_axial_rope(x, cos, sin):
    B, T, Hh, Dh = x.shape
    xs = x.reshape(B, T, Hh, 2, 2, Dh // 4).astype(jnp.float32)
    x1, x2 = xs[..., 0, :], xs[..., 1, :]
    c = cos[None, :, None]
    s = sin[None, :, None]
    out = jnp.stack([x1 * c - x2 * s, x2 * c + x1 * s], axis=-2)
    return out.reshape(B, T, Hh, Dh).astype(x.dtype)


def sink_softmax_attend(q_blk, keys, vals, sink, mask):
    B, Q = q_blk.shape[:2]
    scores = [jnp.einsum('bqkgd,bskd->bkgqs', q_blk, kk).astype(jnp.float32) * ATTN_SCALE for kk in keys]
    if mask is not None:
        scores[0] = jnp.where(mask[None, None, None], scores[0], NEG_INF)
    sink_col = jnp.broadcast_to(sink.reshape(ATTN_KV_HEADS, ATTN_GROUP)[None, :, :, None, None].astype(jnp.float32),
                                (B, ATTN_KV_HEADS, ATTN_GROUP, Q, 1))
    prob = jax.nn.softmax(jnp.concatenate(scores + [sink_col], axis=-1), axis=-1)
    out = None
    off = 0
    for vv in vals:
        n = vv.shape[1]
        o = jnp.einsum('bkgqs,bskd->bqkgd', prob[..., off:off + n].astype(vv.dtype), vv)
        out = o if out is None else out + o
        off += n
    return out


def attn_context(h, p):
    B, T, _ = h.shape
    q, k, v = attn_qkv(h, p)
    nq = T // BLOCK
    qb = jnp.moveaxis(q.reshape(B, nq, BLOCK, ATTN_KV_HEADS, ATTN_GROUP, ATTN_HEAD_DIM), 1, 0)
    o = lax.map(lambda qblk: sink_softmax_attend(qblk, [k], [v], p['sink'], None), qb)
    o = jnp.moveaxis(o, 0, 1).reshape(B, T, ATTN_HEADS * ATTN_HEAD_DIM)
    return o @ p['w_o'], k, v


def attn_latent(h, p, k_ctx, v_ctx):
    B, T, _ = h.shape
    q, k, v = attn_qkv(h, p)
    cos, sin = axial_rope_tables(T)
    q = apply_axial_rope(q, cos, sin)
    k = apply_axial_rope(k, cos, sin)
    nb = T // BLOCK
    qb = jnp.moveaxis(q.reshape(B, nb, BLOCK, ATTN_KV_HEADS, ATTN_GROUP, ATTN_HEAD_DIM), 1, 0)

    def band(z):
        zp = jnp.pad(z, ((0, 0), (BLOCK, BLOCK), (0, 0), (0, 0))).reshape(B, nb + 2, BLOCK, ATTN_KV_HEADS, ATTN_HEAD_DIM)
        zb = jnp.concatenate([zp[:, :-2], zp[:, 1:-1], zp[:, 2:]], axis=2)
        return jnp.moveaxis(zb, 1, 0)

    kb, vb = band(k), band(v)
    blk = jnp.arange(nb)[:, None, None] * BLOCK
    qpos = blk + jnp.arange(BLOCK)[None, :, None]
    kpos = blk + jnp.arange(3 * BLOCK)[None, None, :] - BLOCK
    valid = (jnp.abs(qpos - kpos) <= WINDOW) & (kpos >= 0) & (kpos < T)
    o = lax.map(lambda a: sink_softmax_attend(a[0], [a[1], k_ctx], [a[2], v_ctx], p['sink'], a[3]),
                (qb, kb, vb, valid))
    o = jnp.moveaxis(o, 0, 1).reshape(B, T, ATTN_HEADS * ATTN_HEAD_DIM)
    return o @ p['w_o']


def conv_ffn(h, up, conv_w, conv_b, down):
    T = h.shape[1]
    u = h @ up
    pad = CONV_WIDTH // 2
    up_ = jnp.pad(u, ((0, 0), (pad, pad), (0, 0)))
    u = sum(up_[:, j:j + T] * conv_w[j] for j in range(CONV_WIDTH)) + conv_b
    gate, val = jnp.split(u, 2, axis=-1)
    return (jax.nn.silu(gate) * val) @ down


def setup_inputs(seed: int = 0) -> dict:
    key = jax.random.key(seed)
    ks = iter(jax.random.split(key, 64))

    def nrm(shape, scale):
        return jax.random.normal(next(ks), shape, jnp.float32) * scale

    D, F = D_MODEL, D_FF
    H, N = RWKV_HEADS, RWKV_HEAD
    KV, Dh = ATTN_KV_HEADS, ATTN_HEAD_DIM
    NP, NR, NA = N_POOL_LAYERS, N_RWKV_LAYERS, N_ATTN_LAYERS
    qkv_w = (ATTN_HEADS + 2 * KV) * Dh
    inp = {}
    inp['x_prompt'] = nrm((BATCH, SEQ, D), 1.0)
    inp['x_sample'] = nrm((DEC_BATCH, DEC_SEQ, D), 1.0)
    inp['state_rwkv'] = nrm((DEC_BATCH, NR, 2, H, N, N), 0.5)
    inp['cache_k'] = nrm((DEC_BATCH, NA, PAST_LEN, KV, Dh), 1.0)
    inp['cache_v'] = nrm((DEC_BATCH, NA, PAST_LEN, KV, Dh), 1.0)
    inp['c'] = nrm((DEC_BATCH, D), 1.0)
    inp['c_ctx'] = nrm((D,), 1.0)
    inp['ada_w'] = nrm((DEPTH, D, 6 * D), D ** -0.5)
    inp['ada_b'] = nrm((DEPTH, 6 * D), 0.02)
    inp['norm_mix'] = 1.0 + nrm((DEPTH, D), 0.02)
    inp['norm_ffn'] = 1.0 + nrm((DEPTH, D), 0.02)
    inp['ffn_up'] = nrm((DEPTH, D, 2 * F), D ** -0.5)
    inp['ffn_conv_w'] = nrm((DEPTH, CONV_WIDTH, 2 * F), CONV_WIDTH ** -0.5)
    inp['ffn_conv_b'] = nrm((DEPTH, 2 * F), 0.02)
    inp['ffn_down'] = nrm((DEPTH, F, D), F ** -0.5)
    inp['pool_w'] = nrm((NP, len(POOL_WINDOWS), POOL_GROUP, POOL_GROUP), POOL_GROUP ** -0.5)
    inp['pool_scale'] = 1.0 + nrm((NP, D), 0.1)
    inp['rwkv_mu'] = jax.random.uniform(next(ks), (NR, 6, D), jnp.float32)
    inp['rwkv_w_r'] = nrm((NR, D, D), D ** -0.5)
    inp['rwkv_w_k'] = nrm((NR, D, D), D ** -0.5)
    inp['rwkv_w_v'] = nrm((NR, D, D), D ** -0.5)
    inp['rwkv_w0'] = nrm((NR, 2, D), 0.5)
    inp['rwkv_w1'] = nrm((NR, 2, D, DECAY_LORA), D ** -0.5)
    inp['rwkv_w2'] = nrm((NR, 2, DECAY_LORA, D), 0.5 * DECAY_LORA ** -0.5)
    inp['rwkv_a0'] = nrm((NR, 2, D), 0.5)
    inp['rwkv_a1'] = nrm((NR, 2, D, AAA_LORA), D ** -0.5)
    inp['rwkv_a2'] = nrm((NR, 2, AAA_LORA, D), 0.5 * AAA_LORA ** -0.5)
    inp['rwkv_g1'] = nrm((NR, D, GATE_LORA), D ** -0.5)
    inp['rwkv_g2'] = nrm((NR, GATE_LORA, D), GATE_LORA ** -0.5)
    inp['rwkv_k_k'] = 0.85 + nrm((NR, D), 0.02)
    inp['rwkv_k_a'] = 1.0 + nrm((NR, D), 0.02)
    inp['rwkv_r_k'] = nrm((NR, H, N), 0.1)
    inp['rwkv_ln_w'] = 1.0 + nrm((NR, D), 0.02)
    inp['rwkv_ln_b'] = nrm((NR, D), 0.02)
    inp['rwkv_w_o'] = nrm((NR, D, D), D ** -0.5)
    inp['attn_w_qkv'] = nrm((NA, D, qkv_w), D ** -0.5)
    inp['attn_q_norm'] = 1.0 + nrm((NA, Dh), 0.02)
    inp['attn_k_norm'] = 1.0 + nrm((NA, Dh), 0.02)
    inp['attn_sink'] = nrm((NA, ATTN_HEADS), 0.5)
    inp['attn_w_o'] = nrm((NA, ATTN_HEADS * Dh, D), (ATTN_HEADS * Dh) ** -0.5)
    return inp


def reference(x_prompt, x_sample, state_rwkv, cache_k, cache_v, c, c_ctx,
              ada_w, ada_b, norm_mix, norm_ffn, ffn_up, ffn_conv_w, ffn_conv_b, ffn_down,
              pool_w, pool_scale,
              rwkv_mu, rwkv_w_r, rwkv_w_k, rwkv_w_v, rwkv_w0, rwkv_w1, rwkv_w2,
              rwkv_a0, rwkv_a1, rwkv_a2, rwkv_g1, rwkv_g2, rwkv_k_k, rwkv_k_a, rwkv_r_k,
              rwkv_ln_w, rwkv_ln_b, rwkv_w_o,
              attn_w_qkv, attn_q_norm, attn_k_norm, attn_sink, attn_w_o):
    xp, xs = x_prompt, x_sample
    rwkv_states, ctx_keys, ctx_vals = [], [], []
    for l in range(DEPTH):
        kind, slot = MIXER_OF_LAYER[l], LAYER_SLOT[l]
        sh1p, sc1p, g1p, sh2p, sc2p, g2p = adaln(c_ctx[None, :], ada_w[l], ada_b[l])
        sh1s, sc1s, g1s, sh2s, sc2s, g2s = adaln(c, ada_w[l], ada_b[l])
        hp = modulate(rmsnorm(xp, norm_mix[l]), sh1p, sc1p)
        hs = modulate(rmsnorm(xs, norm_mix[l]), sh1s, sc1s)
        if kind == 0:
            yp = pool_mixer(hp, pool_w[slot], pool_scale[slot])
            ys = pool_mixer(hs, pool_w[slot], pool_scale[slot])
        elif kind == 1:
            p = {'mu': rwkv_mu[slot], 'w_r': rwkv_w_r[slot], 'w_k': rwkv_w_k[slot], 'w_v': rwkv_w_v[slot],
                 'w0': rwkv_w0[slot], 'w1': rwkv_w1[slot], 'w2': rwkv_w2[slot],
                 'a0': rwkv_a0[slot], 'a1': rwkv_a1[slot], 'a2': rwkv_a2[slot],
                 'g1': rwkv_g1[slot], 'g2': rwkv_g2[slot], 'k_k': rwkv_k_k[slot], 'k_a': rwkv_k_a[slot],
                 'r_k': rwkv_r_k[slot], 'ln_w': rwkv_ln_w[slot], 'ln_b': rwkv_ln_b[slot], 'w_o': rwkv_w_o[slot]}
            zero = jnp.zeros((xp.shape[0], RWKV_HEADS, RWKV_HEAD, RWKV_HEAD), jnp.float32)
            yp, s_f, s_b = rwkv_mixer(hp, p, zero, zero)
            rwkv_states.append(jnp.stack([s_f, s_b], axis=1))
            ys, _, _ = rwkv_mixer(hs, p, state_rwkv[:, slot, 0], state_rwkv[:, slot, 1])
        else:
            p = {'w_qkv': attn_w_qkv[slot], 'q_norm': attn_q_norm[slot], 'k_norm': attn_k_norm[slot],
                 'sink': attn_sink[slot], 'w_o': attn_w_o[slot]}
            yp, k_c, v_c = attn_context(hp, p)
            ctx_keys.append(k_c)
            ctx_vals.append(v_c)
            ys = attn_latent(hs, p, cache_k[:, slot], cache_v[:, slot])
        xp = xp + g1p * yp
        xs = xs + g1s * ys
        hp = modulate(rmsnorm(xp, norm_ffn[l]), sh2p, sc2p)
        hs = modulate(rmsnorm(xs, norm_ffn[l]), sh2s, sc2s)
        xp = xp + g2p * conv_ffn(hp, ffn_up[l], ffn_conv_w[l], ffn_conv_b[l], ffn_down[l])
        xs = xs + g2s * conv_ffn(hs, ffn_up[l], ffn_conv_w[l], ffn_conv_b[l], ffn_down[l])
    y_prompt = xp
    y_sample = xs
    new_state_rwkv = jnp.stack(rwkv_states, axis=1)
    new_cache_k = jnp.stack(ctx_keys, axis=1)
    new_cache_v = jnp.stack(ctx_vals, axis=1)
    return (y_prompt, y_sample, new_state_rwkv, new_cache_k, new_cache_v)
```

```python
from contextlib import ExitStack
import numpy as np
import concourse.bass as bass
import concourse.mybir as mybir
from concourse.bass_utils import run_bass_kernel_spmd

F32 = mybir.dt.float32
BF16 = mybir.dt.bfloat16
AF = mybir.ActivationFunctionType
ALU = mybir.AluOpType

PE, ACT, DVE, POOL, SP = "tensor", "scalar", "vector", "gpsimd", "sync"
ENGS = [PE, ACT, DVE, POOL, SP]
SEM_ROLL = 20000

D = 2048
KC = 16
NTOK = 2560
DFF = 5632
FC = 44
DEPTH = 4
MIXER = (0, 1, 2, 0)
SLOT = (0, 0, 0, 1)
SEGS = [(0, 512, 0), (512, 512, 0), (1024, 512, 0), (1536, 512, 0), (2048, 512, 1)]
SEQS = [(0, 2048, 0), (2048, 256, 1), (2304, 256, 1)]
NORM_EPS = 1e-6


class Buf:
    __slots__ = ("name", "w", "r", "dsem", "const", "excl")

    def __init__(self, name, const=False, excl=False):
        self.name = name
        self.w = None
        self.r = {}
        self.dsem = None
        self.const = const
        self.excl = excl


class Prog:
    def __init__(self, nc, stack):
        self.nc = nc
        self.stack = stack
        self.streams = {e: [] for e in ENGS}
        self.cnt = {e: 0 for e in ENGS}
        self.sems = {}
        self.dma_tot = {}
        self.waited = {e: {} for e in ENGS}
        self.ndma = 0

    def _sem(self, key):
        if key not in self.sems:
            self.sems[key] = self.stack.enter_context(self.nc.semaphore("s_" + "_".join(str(k) for k in key)))
        return self.sems[key]

    def _waits(self, eng, deps):
        best = {}
        for (k, v) in deps:
            if k[0] == "dma":
                v = self.dma_tot[k]
            elif k[1] == PE and eng == PE:
                continue
            if v > best.get(k, 0):
                best[k] = v
        out = []
        wd = self.waited[eng]
        for k, v in best.items():
            if wd.get(k, 0) >= v:
                continue
            wd[k] = v
            out.append((k, v))
        return out

    def _deps(self, reads, writes):
        deps = []
        for b in reads:
            if b.w is not None:
                deps.append(b.w)
            if b.excl:
                deps.extend(b.r.items())
        for b in writes:
            if b.w is not None:
                deps.append(b.w)
            deps.extend(b.r.items())
        return deps

    def _record(self, ev, reads, writes):
        for b in reads:
            if b.excl:
                b.w = ev
                b.r = {}
            elif not b.const:
                if ev[1] > b.r.get(ev[0], 0):
                    b.r[ev[0]] = ev[1]
        for b in writes:
            b.w = ev
            b.r = {}

    def op(self, eng, fn, reads=(), writes=()):
        waits = self._waits(eng, self._deps(reads, writes))
        self.cnt[eng] += 1
        c = self.cnt[eng]
        key = ("eng", eng, (c - 1) // SEM_ROLL)
        ev = (key, (c - 1) % SEM_ROLL + 1)
        self._sem(key)
        self.streams[eng].append((waits, fn, ev, 1))
        self._record(ev, reads, writes)
        return ev

    def dma(self, q, fn, reads=(), writes=(), chan=None):
        waits = self._waits(q, self._deps(reads, writes))
        if chan is None:
            chan = writes[0]
        if chan.dsem is None:
            chan.dsem = ("dma", self.ndma)
            self.ndma += 1
            self.dma_tot[chan.dsem] = 0
            self._sem(chan.dsem)
        key = chan.dsem
        self.dma_tot[key] += 16
        ev = (key, self.dma_tot[key])
        self.streams[q].append((waits, fn, ev, 16))
        self._record(ev, reads, writes)
        return ev

    def barrier(self):
        allev = []
        for e in ENGS:
            c = self.cnt[e]
            if c:
                allev.append((("eng", e, (c - 1) // SEM_ROLL), (c - 1) % SEM_ROLL + 1))
        for k, v in self.dma_tot.items():
            if v:
                allev.append((k, v))
        for e in ENGS:
            waits = self._waits(e, [x for x in allev])
            if waits:
                self.streams[e].append((waits, None, None, 0))

    def emit(self):
        nc = self.nc
        with nc.Block() as block:
            for e in ENGS:
                stream = self.streams[e]

                def body(eng, stream=stream):
                    for (waits, fn, ev, inc) in stream:
                        for (k, v) in waits:
                            eng.wait_ge(self.sems[k], v)
                        if fn is not None:
                            fn(eng).then_inc(self.sems[ev[0]], inc)
                getattr(block, e)(body)


class T:
    def __init__(self, h, name, const=False):
        self.h = h
        self.b = Buf(name, const)

    def __getitem__(self, idx):
        return self.h[idx]


def fm(v):
    v = np.asarray(v, np.float32).reshape(-1, 128)
    return np.ascontiguousarray(v.T)


def vec_layout():
    off = {}
    pos = 0

    def add(name, n):
        nonlocal pos
        off[name] = (pos, n)
        pos += n
    for l in range(DEPTH):
        add(f"adab{l}", 96)
        add(f"nm{l}", 16)
        add(f"nf{l}", 16)
        for j in range(3):
            add(f"cw{l}_{j}", 88)
        add(f"cb{l}", 88)
    for s in range(2):
        add(f"pscale{s}", 16)
    add("pool_icnt", 64)
    add("qn", 1)
    add("kn", 1)
    add("sink", 16)
    for i in range(6):
        add(f"mu{i}", 16)
    for e in range(2):
        add(f"w0_{e}", 16)
        add(f"a0_{e}", 16)
    for nm_ in ("k_k", "k_a", "r_k", "ln_w", "ln_b"):
        add(nm_, 16)
    off["_total"] = (pos, 0)
    return off


VOFF = vec_layout()


def build_vecs(inp):
    tot = VOFF["_total"][0]
    vecs = np.zeros((128, tot), np.float32)

    def put(name, arr):
        o, n = VOFF[name]
        assert arr.shape == (128, n), (name, arr.shape, n)
        vecs[:, o:o + n] = arr
    for l in range(DEPTH):
        put(f"adab{l}", fm(inp["ada_b"][l]))
        put(f"nm{l}", fm(inp["norm_mix"][l]))
        put(f"nf{l}", fm(inp["norm_ffn"][l]))
        for j in range(3):
            put(f"cw{l}_{j}", fm(inp["ffn_conv_w"][l][j]))
        put(f"cb{l}", fm(inp["ffn_conv_b"][l]))
    for s in range(2):
        put(f"pscale{s}", fm(inp["pool_scale"][s]))
    ic = np.zeros((4, 16), np.float32)
    for gi, win in enumerate((2, 4, 8, 16)):
        left = win // 2
        right = win - 1 - left
        for t in range(left):
            ic[gi, t] = 1.0 / (t + right + 1)
        for j in range(right):
            ic[gi, 8 + j] = 1.0 / (j + 1 + left)
    put("pool_icnt", np.broadcast_to(ic.reshape(1, 64), (128, 64)).copy())
    put("qn", np.asarray(inp["attn_q_norm"][0], np.float32).reshape(128, 1))
    put("kn", np.asarray(inp["attn_k_norm"][0], np.float32).reshape(128, 1))
    for i in range(6):
        put(f"mu{i}", fm(inp["rwkv_mu"][0][i]))
    for e in range(2):
        put(f"w0_{e}", fm(inp["rwkv_w0"][0][e]))
        put(f"a0_{e}", fm(inp["rwkv_a0"][0][e]))
    put("k_k", fm(inp["rwkv_k_k"][0]))
    put("k_a", fm(inp["rwkv_k_a"][0]))
    put("r_k", fm(inp["rwkv_r_k"][0].reshape(-1)))
    put("ln_w", fm(inp["rwkv_ln_w"][0]))
    put("ln_b", fm(inp["rwkv_ln_b"][0]))
    put("sink", np.broadcast_to(np.asarray(inp["attn_sink"][0], np.float32).reshape(1, 16), (128, 16)).copy())
    return vecs


def build(layers=(0, 1, 2, 3), mixers=True, ffn=True):
    nc = bass.Bass("TRN2", target_bir_lowering=False)
    dr = lambda name, shape, dt=F32, kind="ExternalInput": nc.dram_tensor(name, list(shape), dt, kind=kind).ap()
    xin = dr("xin", [NTOK, D])
    condT = dr("condT", [128, KC, 2])
    vecs_d = dr("vecs", [128, VOFF["_total"][0]])
    ident_d = dr("ident", [128, 128])
    ada_w = dr("ada_w", [DEPTH, D, 6 * D])
    ffn_up = dr("ffn_up", [DEPTH, D, 2 * DFF])
    ffn_down = dr("ffn_down", [DEPTH, DFF, D])
    pool_w = dr("pool_w", [2, 4, 512, 512])
    w_qkv = dr("w_qkv", [D, 3072])
    w_ao = dr("w_ao", [D, D])
    cache_k = dr("cache_k", [512, 512])
    cache_v = dr("cache_v", [512, 512])
    amask_d = dr("amask", [128, 6, 512], BF16)
    rotT_d = dr("rotT", [128, 128])
    ropeC_d = dr("ropeC", [128, 2048])
    ropeS_d = dr("ropeS", [128, 2048])
    ck_out = dr("ck_out", [2, 256, 512], kind="ExternalOutput")
    rw_r = dr("rw_r", [D, D])
    rw_k = dr("rw_k", [D, D])
    rw_v = dr("rw_v", [D, D])
    rw_o = dr("rw_o", [D, D])
    rw_w1 = dr("rw_w1", [2, D, 96])
    rw_w2 = dr("rw_w2", [2, 96, D])
    rw_a1 = dr("rw_a1", [2, D, 96])
    rw_a2 = dr("rw_a2", [2, 96, D])
    rw_g1 = dr("rw_g1", [D, 256])
    rw_g2 = dr("rw_g2", [256, D])
    st_in = dr("st_in", [2, D, 64])
    rmask_d = dr("rmask", [4, 128, 4, 128], BF16)
    identb_d = dr("identb", [128, 4, 128], BF16)
    bones_d = dr("bones", [128, 128])
    reset_d = dr("resetm", [128, 256])
    st_out = dr("st_out", [2, 2, D, 64], kind="ExternalOutput")
    import os as _os2
    _dbg = _os2.environ.get("RW_DEBUG") == "1"
    osc = dr("osc", [KC, 128, NTOK], kind="ExternalOutput" if _dbg else "Internal").rearrange("c p t -> p c t")
    bsc = dr("bsc", [KC, 128, NTOK], kind="ExternalOutput" if _dbg else "Internal").rearrange("c p t -> p c t")
    cv_out = dr("cv_out", [2, 256, 512], kind="ExternalOutput")
    y = dr("y", [NTOK, D], kind="ExternalOutput")
    scr2 = [dr(f"scr{i}", [KC, 128, NTOK], kind="Internal") for i in range(2)]
    scr_v2 = [s_.rearrange("c p t -> p c t") for s_ in scr2]

    with ExitStack() as st:
        P = Prog(nc, st)

        uniq = [0]

        def sb(name, shape, dt=F32, const=False, stack=st):
            uniq[0] += 1
            return T(stack.enter_context(nc.sbuf_tensor(f"t{uniq[0]}_{name}", list(shape), dt)), name, const)

        ident = sb("ident", [128, 128], F32)
        ones = sb("ones", [128, 128], F32)
        vecs = sb("vecs", [128, VOFF["_total"][0]], F32)
        cond = sb("cond", [128, KC, 2], F32)
        scond = sb("scond", [128, KC, 2], BF16)
        mod = sb("mod", [128, 96, 2], F32)
        gs1 = sb("gs1", [128, KC, 2], F32)
        gs2 = sb("gs2", [128, KC, 2], F32)
        gpl = sb("gpl", [128, KC, 2], F32)
        psum = [T(st.enter_context(nc.psum_tensor(f"ps{i}", [128, 512], F32)), f"ps{i}") for i in range(8)]
        for p_ in psum:
            p_.b.excl = True
        scr_b2 = [[Buf(f"scr{i}_{s}") for s in range(5)] for i in range(2)]
        scur = 0
        scr_v, scr_b = scr_v2[0], scr_b2[0]
        y_b = Buf("y")

        def V(name, j=None, n=None):
            o, nn = VOFF[name]
            if j is None:
                return vecs[:, o:o + nn]
            return vecs[:, o + j:o + j + (n or 1)]

        def mm(out, lhsT, rhs, start, stop, R, W):
            P.op(PE, lambda e: e.matmul(out, lhsT=lhsT, rhs=rhs, start=start, stop=stop), reads=R, writes=W)

        def tr(out, in_, R, W):
            P.op(PE, lambda e: e.transpose(out=out, in_=in_, identity=ident[:]), reads=R + [ident.b], writes=W)

        def act(out, in_, func, R, W, scale=1.0, bias=0.0):
            P.op(ACT, lambda e: e.activation(out=out, in_=in_, func=func, bias=bias, scale=scale), reads=R, writes=W)

        def tt(eng, out, in0, in1, op, R, W):
            P.op(eng, lambda e: e.tensor_tensor(out=out, in0=in0, in1=in1, op=op), reads=R, writes=W)

        def ts(eng, out, in0, s1, op0, R, W, s2=None, op1=None):
            if op1 is None:
                P.op(eng, lambda e: e.tensor_scalar(out=out, in0=in0, scalar1=s1, scalar2=None, op0=op0), reads=R, writes=W)
            else:
                P.op(eng, lambda e: e.tensor_scalar(out=out, in0=in0, scalar1=s1, scalar2=s2, op0=op0, op1=op1), reads=R, writes=W)

        def stt(out, in0, scalar, in1, op0, op1, R, W):
            P.op(DVE, lambda e: e.scalar_tensor_tensor(out=out, in0=in0, scalar=scalar, in1=in1, op0=op0, op1=op1), reads=R, writes=W)

        def cp(eng, out, in_, R, W):
            if eng == ACT:
                act(out, in_, AF.Copy, R, W)
            else:
                P.op(eng, lambda e: e.tensor_copy(out=out, in_=in_), reads=R, writes=W)

        def memset(eng, ap, val, W):
            P.op(eng, lambda e: e.memset(ap, val), writes=W)

        def dma(q, out, in_, R, W, chan=None):
            P.dma(q, lambda e: e.dma_start(out=out, in_=in_), reads=R, writes=W, chan=chan)

        dma(SP, ident[:], ident_d, [], [ident.b])
        dma(SP, vecs[:], vecs_d, [], [vecs.b])
        dma(SP, cond[:], condT, [], [cond.b])
        memset(DVE, ones[:], 1.0, [ones.b])
        act(scond[:], cond[:], AF.Silu, [cond.b], [scond.b])

        with ExitStack() as ph:
            xt = [sb(f"xt{i}", [128, D], F32, stack=ph) for i in range(2)]
            xst = [sb(f"xst{i}", [128, KC, 128], F32, stack=ph) for i in range(2)]
            for t_ in range(NTOK // 128):
                a, s_ = xt[t_ % 2], xst[t_ % 2]
                dma(SP, a[:], xin[t_ * 128:(t_ + 1) * 128, :], [], [a.b])
                for g in range(4):
                    pb = psum[(t_ * 4 + g) % 8]
                    for j in range(4):
                        kc = g * 4 + j
                        tr(pb[:, j * 128:(j + 1) * 128], a[:, kc * 128:(kc + 1) * 128], [a.b], [pb.b])
                    cp(ACT if g % 2 == 0 else DVE, s_[:, g * 4:(g + 1) * 4, :],
                       pb[:].rearrange("p (a b) -> p a b", a=4), [pb.b], [s_.b])
                dma(SP, scr_v[:, :, t_ * 128:(t_ + 1) * 128], s_[:], [s_.b], [scr_b[t_ // 4]], chan=s_.b)
        P.barrier()

        def rstd_cols(xv, ncols, r_ap, r_b, x_b, sqt, pb):
            for kc in range(KC):
                q = sqt[kc % 2]
                act(q[:, 0:ncols], xv(kc), AF.Square, [x_b], [q.b])
                mm(pb[:, 0:ncols], ones[:], q[:, 0:ncols], kc == 0, kc == KC - 1, [ones.b, q.b], [pb.b])
            act(r_ap, pb[:, 0:ncols], AF.Sqrt, [pb.b], [r_b], scale=1.0 / D, bias=epsb[:, 0:1])
            P.op(DVE, lambda e: e.reciprocal(out=r_ap, in_=r_ap), reads=[r_b], writes=[r_b])

        epsb = sb("epsb", [128, 1], F32)
        memset(DVE, epsb[:], NORM_EPS, [epsb.b])
        gneps = sb("gneps", [128, 1], F32)
        memset(DVE, gneps[:], 64e-5, [gneps.b])

        for l in layers:
            kind, slot = MIXER[l], SLOT[l]
            with ExitStack() as ph:
                wts = [sb(f"aw{i}", [128, KC, 512], BF16, stack=ph) for i in range(3)]
                aw_v = ada_w[l].rearrange("(c p) n -> p c n", p=128)
                NST = 24

                def aload(i):
                    w = wts[i % 3]
                    dma(POOL, w[:], aw_v[:, :, i * 512:(i + 1) * 512], [], [w.b])
                aload(0)
                for i in range(NST):
                    if i + 1 < NST:
                        aload(i + 1)
                    w = wts[i % 3]
                    pb = psum[i % 4]
                    for o4 in range(4):
                        for kc in range(KC):
                            mm(pb[:, o4 * 2:o4 * 2 + 2], w[:, kc, o4 * 128:(o4 + 1) * 128], scond[:, kc, :],
                               kc == 0, kc == KC - 1, [w.b, scond.b], [pb.b])
                    for o4 in range(4):
                        oc = i * 4 + o4
                        act(mod[:, oc, :], pb[:, o4 * 2:o4 * 2 + 2], AF.Identity, [pb.b, vecs.b], [mod.b],
                            bias=V(f"adab{l}", oc))
                for cc in range(2):
                    stt(gs1[:, :, cc], mod[:, 16:32, cc], 1.0, V(f"nm{l}"), ALU.add, ALU.mult, [mod.b, vecs.b], [gs1.b])
                    stt(gs2[:, :, cc], mod[:, 64:80, cc], 1.0, V(f"nf{l}"), ALU.add, ALU.mult, [mod.b, vecs.b], [gs2.b])
                    if kind == 0:
                        tt(DVE, gpl[:, :, cc], mod[:, 32:48, cc], V(f"pscale{slot}"), ALU.mult, [mod.b, vecs.b], [gpl.b])
            P.barrier()

            if mixers and kind == 0:
                with ExitStack() as ph:
                    r_all = sb("r_all", [128, NTOK], F32, stack=ph)
                    xs = sb("pxs", [128, KC, 512], F32, stack=ph)
                    sqt = [sb(f"psq{i}", [128, 512], F32, stack=ph) for i in range(2)]
                    for si, (s0, sl, cc) in enumerate(SEGS):
                        dma(SP, xs[:], scr_v[:, :, s0:s0 + sl], [scr_b[si]], [xs.b])
                        rstd_cols(lambda kc: xs[:, kc, :], 512, r_all[:, s0:s0 + sl], r_all.b, xs.b, sqt, psum[si % 2])
                    xg = sb("xg", [128, 4, NTOK], F32, stack=ph)
                    pooled = sb("pooled", [128, 4, NTOK], BF16, stack=ph)
                    hp = [sb(f"hp{i}", [128, 2064], F32, stack=ph) for i in range(2)]
                    tmp = [sb(f"ptmp{i}", [128, 2048], F32, stack=ph) for i in range(2)]
                    sA = sb("sA", [128, 2064], F32, stack=ph)
                    sB = sb("sB", [128, 2064], F32, stack=ph)
                    pw = [sb(f"pw{i}", [128, 4, 512], BF16, stack=ph) for i in range(2)]
                    it = 0
                    for gi, win in enumerate((2, 4, 8, 16)):
                        left = win // 2
                        right = win - 1 - left
                        w = pw[gi % 2]
                        dma(POOL, w[:], pool_w[slot, gi].rearrange("(c p) n -> p c n", p=128), [], [w.b])
                        dma(SP, xg[:], scr_v[:, 4 * gi:4 * gi + 4, :], scr_b, [xg.b])
                        for k4 in range(4):
                            kc = 4 * gi + k4
                            for (q0, Tn, cc) in SEQS:
                                h = hp[it % 2]
                                tm = tmp[it % 2]
                                it += 1
                                Wd = Tn + 16
                                memset(DVE, h[:, 0:8], 0.0, [h.b])
                                memset(DVE, h[:, Tn + 8:Tn + 16], 0.0, [h.b])
                                stt(tm[:, 0:Tn], xg[:, k4, q0:q0 + Tn], gs1[:, kc, cc:cc + 1], r_all[:, q0:q0 + Tn],
                                    ALU.mult, ALU.mult, [xg.b, gs1.b, r_all.b], [tm.b])
                                act(h[:, 8:8 + Tn], tm[:, 0:Tn], AF.Identity, [tm.b, mod.b], [h.b], bias=mod[:, kc, cc:cc + 1])
                                tt(DVE, sA[:, 1:Wd], h[:, 1:Wd], h[:, 0:Wd - 1], ALU.add, [h.b], [sA.b])
                                cur, lo, hi = sA, 1, Wd
                                oth = sB
                                for sh_ in (1, 2, 4):
                                    if win <= 2 * sh_:
                                        break
                                    tt(DVE, oth[:, lo + sh_:hi - sh_], cur[:, lo:hi - 2 * sh_], cur[:, lo + 2 * sh_:hi], ALU.add,
                                       [cur.b], [oth.b])
                                    cur, oth = oth, cur
                                    lo, hi = lo + sh_, hi - sh_
                                po = pooled[:, k4, q0:q0 + Tn]
                                stt(po, cur[:, 8:8 + Tn], 1.0 / win, h[:, 8:8 + Tn], ALU.mult, ALU.subtract, [cur.b, h.b], [pooled.b])
                                o_ic = VOFF["pool_icnt"][0] + gi * 16
                                tt(DVE, tm[:, 0:left], cur[:, 8:8 + left], vecs[:, o_ic:o_ic + left], ALU.mult, [cur.b, vecs.b], [tm.b])
                                tt(DVE, pooled[:, k4, q0:q0 + left], tm[:, 0:left], h[:, 8:8 + left], ALU.subtract, [tm.b, h.b], [pooled.b])
                                if right > 0:
                                    for j in range(right):
                                        c_ = Tn - 1 - j
                                        stt(pooled[:, k4, q0 + c_:q0 + c_ + 1], cur[:, 8 + c_:8 + c_ + 1], vecs[:, o_ic + 8 + j:o_ic + 9 + j],
                                            h[:, 8 + c_:8 + c_ + 1], ALU.mult, ALU.subtract, [cur.b, h.b, vecs.b], [pooled.b])
                        n_ = 0
                        for o4 in range(4):
                            kc = 4 * gi + o4
                            for si, (s0, sl, cc) in enumerate(SEGS):
                                pb = psum[2 + n_ % 6]
                                n_ += 1
                                for ic in range(4):
                                    mm(pb[:, 0:sl], w[:, ic, o4 * 128:(o4 + 1) * 128], pooled[:, ic, s0:s0 + sl], ic == 0, ic == 3,
                                       [w.b, pooled.b], [pb.b])
                                stt(xg[:, o4, s0:s0 + sl], pb[:, 0:sl], gpl[:, kc, cc:cc + 1], xg[:, o4, s0:s0 + sl], ALU.mult, ALU.add,
                                    [pb.b, gpl.b, xg.b], [xg.b])
                        dma(SP, scr_v[:, 4 * gi:4 * gi + 4, :], xg[:], [xg.b], scr_b, chan=xg.b)
                P.barrier()


            if mixers and kind == 1:
                with ExitStack() as ph:
                    CD = 0.606531
                    NB = 256

                    class TV:
                        def __init__(self, ap, b_):
                            self.ap, self.b = ap, b_

                        def __getitem__(self, idx):
                            return self.ap[idx]
                    rmask = [sb(f"w_mask{i}", [128, 4, 128], BF16, stack=ph) for i in range(4)]
                    identb = sb("w_identb", [128, 4, 128], BF16, stack=ph)
                    bones = sb("w_bones", [128, 128], F32, stack=ph)
                    resetm = sb("w_reset", [128, NB], F32, stack=ph)
                    omka = sb("w_omka", [128, KC], F32, stack=ph)
                    for i in range(4):
                        dma(SP, rmask[i][:], rmask_d[i], [], [rmask[i].b])
                    dma(SP, identb[:], identb_d, [], [identb.b])
                    dma(SP, bones[:], bones_d, [], [bones.b])
                    dma(SP, resetm[:], reset_d, [], [resetm.b])
                    ts(DVE, omka[:], V("k_a"), -1.0, ALU.mult, [vecs.b], [omka.b], s2=1.0, op1=ALU.add)
                    Mf = sb("w_Mf", [128, KC, 128], F32, stack=ph)
                    Mb = sb("w_Mb", [128, KC, 128], BF16, stack=ph)
                    xsraw = sb("w_xs", [128, KC * (NB + 2)], F32, stack=ph)
                    xs = TV(xsraw.h[:, :].rearrange("p (k n) -> p k n", k=KC), xsraw.b)
                    yb = TV(xsraw.h.bitcast(BF16)[:, 0:KC * NB].rearrange("p (k n) -> p k n", k=KC), xsraw.b)
                    rr = sb("w_rr", [128, NB + 2], F32, stack=ph)
                    sqt = [sb(f"w_sq{i}", [128, NB + 2], F32, stack=ph) for i in range(2)]
                    hk = [sb(f"w_hk{i}", [128, NB + 2], F32, stack=ph) for i in range(2)]
                    t1k = [sb(f"w_t1k{i}", [128, NB + 2], F32, stack=ph) for i in range(2)]
                    xxk = [sb(f"w_xxk{i}", [128, NB], F32, stack=ph) for i in range(2)]
                    Xr = sb("w_Xr", [128, KC, NB], BF16, stack=ph)
                    Xk = sb("w_Xk", [128, KC, NB], BF16, stack=ph)
                    Xv = sb("w_Xv", [128, KC, NB], BF16, stack=ph)
                    Xsm = [[sb(f"w_Xs{j}{i}", [128, NB], BF16, stack=ph) for i in range(2)] for j in range(3)]
                    Lw = sb("w_Lw", [96, NB], BF16, stack=ph)
                    La = sb("w_La", [96, NB], BF16, stack=ph)
                    Lg = sb("w_Lg", [128, 2, NB], BF16, stack=ph)
                    w1t = sb("w_w1t", [128, KC, 96], BF16, stack=ph)
                    a1t = sb("w_a1t", [128, KC, 96], BF16, stack=ph)
                    w2t = sb("w_w2t", [96, 512], BF16, stack=ph)
                    a2t = sb("w_a2t", [96, 512], BF16, stack=ph)
                    g2t = sb("w_g2t", [128, 2, 512], BF16, stack=ph)
                    wbig = [sb(f"w_wbig{i}", [128, KC, 256], BF16, stack=ph) for i in range(3)]
                    xres = [sb(f"w_xres{i}", [128, NB], F32, stack=ph) for i in range(2)]
                    pts = [{n_: sb(f"w_p{j_}_" + n_, [128, NB], F32, stack=ph) for n_ in
                            ("rf", "kf", "vf", "sg", "af", "kk", "kd", "cum", "Ep", "Em", "Eex", "u1", "u2", "u3")} for j_ in range(2)]
                    pt = pts[0]
                    gC = sb("w_gC", [128, 4, 4], F32, stack=ph)
                    BDn = ("aT", "bT", "kT", "rT", "vT")
                    BD = {n_: sb("w_bd_" + n_, [128, 4, 4, 128], BF16, stack=ph) for n_ in BDn}
                    for n_ in BDn:
                        memset(DVE, BD[n_][:], 0.0, [BD[n_].b])
                    o_fm = sb("w_ofm", [128, 4, NB], F32, stack=ph)
                    bon = sb("w_bon", [128, 4, NB], F32, stack=ph)
                    NSET = 2
                    cs_b16 = ("A_tm", "B_tm", "K_tm", "V_tm", "P0", "P1", "Q0", "Q1", "N_ak", "N_rb", "N_rk", "Xb", "Xc")
                    CS = [{n_: sb(f"w_c{i}_{n_}", [128, 4, 128], BF16, stack=ph) for n_ in cs_b16} for i in range(NSET)]
                    alias = {"X1": "P0", "Ul": "P1", "Ah": "Q0", "RhT": "Q1", "GpT": "N_ak"}
                    stt_in = sb("w_stin", [128, 64], F32, stack=ph)
                    sbd = sb("w_sbd", [128, 128], F32, stack=ph)
                    memset(DVE, sbd[:], 0.0, [sbd.b])
                    so = [sb(f"w_so{i}", [128, 128], F32, stack=ph) for i in range(2)]
                    st_b = Buf("st_out")
                    osc_b, bsc_b = Buf("osc"), Buf("bsc")
                    pctr = [0]

                    def pnext():
                        pb = psum[pctr[0] % 8]
                        pctr[0] += 1
                        return pb

                    def mm4(pb, L, R, start=True, stop=True, Lfix=None, Rfix=None):
                        for qi in range(4):
                            l_ap = L[0][:, qi, :] if Lfix is None else Lfix[qi]
                            r_ap = R[0][:, qi, :] if Rfix is None else Rfix[qi]
                            mm(pb[:, qi * 128:(qi + 1) * 128], l_ap, r_ap, start, stop, [L[1], R[1]], [pb.b])

                    def mm4multi(pb, terms):
                        for qi in range(4):
                            for ti, (L, R, Lfix, Rfix) in enumerate(terms):
                                l_ap = L[0][:, qi, :] if Lfix is None else Lfix[qi]
                                r_ap = R[0][:, qi, :] if Rfix is None else Rfix[qi]
                                mm(pb[:, qi * 128:(qi + 1) * 128], l_ap, r_ap, ti == 0, ti == len(terms) - 1, [L[1], R[1]], [pb.b])

                    def interleave(gens):
                        gens = list(gens)
                        while gens:
                            for g_ in list(gens):
                                try:
                                    next(g_)
                                except StopIteration:
                                    gens.remove(g_)

                    def pv(pb):
                        return pb[:, 0:512].rearrange("p (q n) -> p q n", q=4)

                    def half(hh):
                        return slice(hh * 64, (hh + 1) * 64)

                    blocks = [(0, 2048, t0) for t0 in range(0, 2048, NB)] + [(2048, 256, 0), (2304, 256, 0)]
                    wr_v = rw_r.rearrange("(c p) n -> p c n", p=128)
                    wk_v = rw_k.rearrange("(c p) n -> p c n", p=128)
                    wv_v = rw_v.rearrange("(c p) n -> p c n", p=128)
                    wo_v = rw_o.rearrange("(c p) n -> p c n", p=128)
                    g1_v = rw_g1.rearrange("(c p) n -> p c n", p=128)
                    wbc = [0]

                    def wbnext():
                        w = wbig[wbc[0] % 3]
                        wbc[0] += 1
                        return w

                    for e in range(2):
                        dma(POOL, w1t[:], rw_w1[e].rearrange("(c p) n -> p c n", p=128), [], [w1t.b])
                        dma(POOL, a1t[:], rw_a1[e].rearrange("(c p) n -> p c n", p=128), [], [a1t.b])
                        mBef, mBefT, mInc = (rmask[0], rmask[1], rmask[2]) if e == 0 else (rmask[1], rmask[0], rmask[3])
                        order = blocks if e == 0 else (blocks[:8][::-1] + blocks[8:])
                        for (q0, Tn, t0) in order:
                            cc = 0 if q0 == 0 else 1
                            g0 = q0 + t0
                            first_of_seq = (t0 == 0) if e == 0 else (t0 + NB == Tn)
                            last_of_seq = (t0 + NB == Tn) if e == 0 else (t0 == 0)
                            if first_of_seq:
                                if cc == 0:
                                    for p_ in range(KC):
                                        dma(SP, stt_in[:], st_in[e, p_ * 128:(p_ + 1) * 128, :], [], [stt_in.b])
                                        for hh in range(2):
                                            cp(DVE, sbd[half(hh), hh * 64:(hh + 1) * 64], stt_in[half(hh), :], [stt_in.b], [sbd.b])
                                        pb = pnext()
                                        tr(pb[:, 0:128], sbd[:], [sbd.b], [pb.b])
                                        cp(ACT, Mf[:, p_, :], pb[:, 0:128], [pb.b], [Mf.b])
                                    cp(DVE, Mb[:], Mf[:], [Mf.b], [Mb.b])
                                else:
                                    memset(DVE, Mf[:], 0.0, [Mf.b])
                                    memset(DVE, Mb[:], 0.0, [Mb.b])
                            lo_t, hi_t = max(t0 - 1, 0), min(t0 + NB + 1, Tn)
                            lo_c = 1 - (t0 - lo_t)
                            hi_c = lo_c + (hi_t - lo_t)
                            dma(SP, xs[:, :, lo_c:hi_c], scr_v[:, :, q0 + lo_t:q0 + hi_t], scr_b, [xs.b])
                            rstd_cols(lambda kc: xs[:, kc, lo_c:hi_c], hi_c - lo_c, rr[:, lo_c:hi_c], rr.b, xs.b, sqt, pnext())
                            if e == 1:
                                g1t = wbnext()
                                dma(POOL, g1t[:], g1_v, [], [g1t.b])
                            p_lw, p_la, p_lg = psum[4], psum[5], (psum[6], psum[7])
                            for kc in range(KC):
                                h_, t1_, xx_ = hk[kc % 2], t1k[kc % 2], xxk[kc % 2]
                                stt(t1_[:, lo_c:hi_c], xs[:, kc, lo_c:hi_c], gs1[:, kc, cc:cc + 1], rr[:, lo_c:hi_c], ALU.mult, ALU.mult,
                                    [xs.b, gs1.b, rr.b], [t1_.b])
                                act(h_[:, lo_c:hi_c], t1_[:, lo_c:hi_c], AF.Identity, [t1_.b, mod.b], [h_.b], bias=mod[:, kc, cc:cc + 1])
                                if lo_c == 1:
                                    memset(DVE, h_[:, 0:1], 0.0, [h_.b])
                                if hi_c == NB + 1:
                                    memset(DVE, h_[:, NB + 1:NB + 2], 0.0, [h_.b])
                                tt(DVE, t1_[:, 0:NB], h_[:, 0:NB], h_[:, 2:NB + 2], ALU.add, [h_.b], [t1_.b])
                                stt(xx_[:], t1_[:, 0:NB], 0.5, h_[:, 1:NB + 1], ALU.mult, ALU.subtract, [t1_.b, h_.b], [xx_.b])
                                for (mi, Xt) in ((0, Xr), (2, Xk), (3, Xv)):
                                    stt(Xt[:, kc, :], xx_[:], V(f"mu{mi}", kc), h_[:, 1:NB + 1], ALU.mult, ALU.add, [xx_.b, vecs.b, h_.b], [Xt.b])
                                xw_, xa_, xg_ = Xsm[0][kc % 2], Xsm[1][kc % 2], Xsm[2][kc % 2]
                                stt(xw_[:], xx_[:], V("mu1", kc), h_[:, 1:NB + 1], ALU.mult, ALU.add, [xx_.b, vecs.b, h_.b], [xw_.b])
                                mm(p_lw[0:96, 0:NB], w1t[:, kc, :], xw_[:], kc == 0, kc == KC - 1, [w1t.b, xw_.b], [p_lw.b])
                                stt(xa_[:], xx_[:], V("mu4", kc), h_[:, 1:NB + 1], ALU.mult, ALU.add, [xx_.b, vecs.b, h_.b], [xa_.b])
                                mm(p_la[0:96, 0:NB], a1t[:, kc, :], xa_[:], kc == 0, kc == KC - 1, [a1t.b, xa_.b], [p_la.b])
                                if e == 1:
                                    stt(xg_[:], xx_[:], V("mu5", kc), h_[:, 1:NB + 1], ALU.mult, ALU.add, [xx_.b, vecs.b, h_.b], [xg_.b])
                                    for j2 in range(2):
                                        mm(p_lg[j2][:, 0:NB], g1t[:, kc, j2 * 128:(j2 + 1) * 128], xg_[:], kc == 0, kc == KC - 1, [g1t.b, xg_.b], [p_lg[j2].b])
                            act(Lw[:], p_lw[0:96, 0:NB], AF.Tanh, [p_lw.b], [Lw.b])
                            cp(ACT, La[:], p_la[0:96, 0:NB], [p_la.b], [La.b])
                            if e == 1:
                                for j2 in range(2):
                                    act(Lg[:, j2, :], p_lg[j2][:, 0:NB], AF.Sigmoid, [p_lg[j2].b], [Lg.b])
                            for pg in range(4):
                                cols = slice(pg * 512, (pg + 1) * 512)
                                dma(POOL, w2t[:], rw_w2[e][:, cols], [], [w2t.b])
                                dma(POOL, a2t[:], rw_a2[e][:, cols], [], [a2t.b])
                                if e == 1:
                                    dma(POOL, g2t[:], rw_g2.rearrange("(c p) n -> p c n", p=128)[:, :, cols], [], [g2t.b])
                                    dma(SP, o_fm[:], osc[:, pg * 4:pg * 4 + 4, g0:g0 + NB], [osc_b], [o_fm.b])
                                    dma(SP, bon[:], bsc[:, pg * 4:pg * 4 + 4, g0:g0 + NB], [bsc_b], [bon.b])
                                def prep(oc, Wr, Wk, Wv, pt):
                                    kcg = pg * 4 + oc
                                    osl = slice((oc % 2) * 128, (oc % 2 + 1) * 128)
                                    osl5 = slice(oc * 128, (oc + 1) * 128)
                                    p_r, p_k, p_v = pnext(), pnext(), pnext()
                                    for kc in range(KC):
                                        mm(p_r[:, 0:NB], Wr[:, kc, osl], Xr[:, kc, :], kc == 0, kc == KC - 1, [Wr.b, Xr.b], [p_r.b])
                                    for kc in range(KC):
                                        mm(p_k[:, 0:NB], Wk[:, kc, osl], Xk[:, kc, :], kc == 0, kc == KC - 1, [Wk.b, Xk.b], [p_k.b])
                                    p_w, p_a = pnext(), pnext()
                                    mm(p_w[:, 0:NB], w2t[:, osl5], Lw[:], True, True, [w2t.b, Lw.b], [p_w.b])
                                    mm(p_a[:, 0:NB], a2t[:, osl5], La[:], True, True, [a2t.b, La.b], [p_a.b])
                                    for kc in range(KC):
                                        mm(p_v[:, 0:NB], Wv[:, kc, osl], Xv[:, kc, :], kc == 0, kc == KC - 1, [Wv.b, Xv.b], [p_v.b])
                                    cp(ACT, pt["kf"][:], p_k[:, 0:NB], [p_k.b], [pt["kf"].b])
                                    act(pt["sg"][:], p_w[:, 0:NB], AF.Sigmoid, [p_w.b, vecs.b], [pt["sg"].b], bias=V(f"w0_{e}", kcg))
                                    act(pt["af"][:], p_a[:, 0:NB], AF.Sigmoid, [p_a.b, vecs.b], [pt["af"].b], bias=V(f"a0_{e}", kcg))
                                    cp(ACT, pt["rf"][:], p_r[:, 0:NB], [p_r.b], [pt["rf"].b])
                                    cp(ACT, pt["vf"][:], p_v[:, 0:NB], [p_v.b], [pt["vf"].b])
                                    yield
                                    ts(DVE, pt["u1"][:], pt["kf"][:], V("k_k", kcg), ALU.mult, [pt["kf"].b, vecs.b], [pt["u1"].b])
                                    tt(DVE, pt["u2"][:], pt["u1"][:], pt["u1"][:], ALU.mult, [pt["u1"].b], [pt["u2"].b])
                                    pb = pnext()
                                    mm(pb[:, 0:NB], bones[:], pt["u2"][:], True, True, [bones.b, pt["u2"].b], [pb.b])
                                    P.op(DVE, lambda e_: e_.tensor_tensor_scan(out=pt["cum"][:], data0=resetm[:], data1=pt["sg"][:], initial=0.0,
                                                                                  op0=ALU.mult, op1=ALU.add),
                                         reads=[resetm.b, pt["sg"].b], writes=[pt["cum"].b])
                                    yield
                                    cum3 = pt["cum"][:].rearrange("p (c t) -> p c t", t=64)
                                    if e == 1:
                                        tt(DVE, pt["Ep"][:].rearrange("p (c t) -> p c t", t=64), cum3[:, :, 63:64].to_broadcast([128, 4, 64]), cum3, ALU.subtract,
                                           [pt["cum"].b], [pt["Ep"].b])
                                        tt(DVE, pt["cum"][:], pt["Ep"][:], pt["sg"][:], ALU.add, [pt["Ep"].b, pt["sg"].b], [pt["cum"].b])
                                        totv = cum3[:, :, 0]
                                    else:
                                        totv = cum3[:, :, 63]
                                    tt(DVE, pt["Eex"][:], pt["cum"][:], pt["sg"][:], ALU.subtract, [pt["cum"].b, pt["sg"].b], [pt["Eex"].b])
                                    act(gC[:, oc, :], totv, AF.Exp, [pt["cum"].b], [gC.b], scale=-CD)
                                    act(pt["Ep"][:], pt["cum"][:], AF.Exp, [pt["cum"].b], [pt["Ep"].b], scale=-CD)
                                    act(pt["Em"][:], pt["cum"][:], AF.Exp, [pt["cum"].b], [pt["Em"].b], scale=CD)
                                    act(pt["Eex"][:], pt["Eex"][:], AF.Exp, [pt["Eex"].b], [pt["Eex"].b], scale=-CD)
                                    ts(DVE, pt["u2"][:], pt["af"][:], V("k_a", kcg), ALU.mult, [pt["af"].b, vecs.b, omka.b], [pt["u2"].b],
                                       s2=omka[:, kcg:kcg + 1], op1=ALU.add)
                                    tt(DVE, pt["kd"][:], pt["kf"][:], pt["u2"][:], ALU.mult, [pt["kf"].b, pt["u2"].b], [pt["kd"].b])
                                    yield
                                    act(pt["u3"][:], pb[:, 0:NB], AF.Sqrt, [pb.b], [pt["u3"].b])
                                    stt(pt["u2"][:], pt["rf"][:], V("r_k", kcg), pt["kd"][:], ALU.mult, ALU.mult, [pt["rf"].b, vecs.b, pt["kd"].b], [pt["u2"].b])
                                    p_b = pnext()
                                    mm(p_b[:, 0:NB], bones[:], pt["u2"][:], True, True, [bones.b, pt["u2"].b], [p_b.b])
                                    ts(DVE, pt["u3"][:], pt["u3"][:], 1e-12, ALU.max, [pt["u3"].b], [pt["u3"].b])
                                    P.op(DVE, lambda e_: e_.reciprocal(out=pt["u3"][:], in_=pt["u3"][:]), reads=[pt["u3"].b], writes=[pt["u3"].b])
                                    tt(DVE, pt["kk"][:], pt["u1"][:], pt["u3"][:], ALU.mult, [pt["u1"].b, pt["u3"].b], [pt["kk"].b])
                                    yield
                                    if e == 0:
                                        tt(DVE, bon[:, oc, :], p_b[:, 0:NB], pt["vf"][:], ALU.mult, [p_b.b, pt["vf"].b], [bon.b])
                                    else:
                                        tt(DVE, pt["u2"][:], p_b[:, 0:NB], pt["vf"][:], ALU.mult, [p_b.b, pt["vf"].b], [pt["u2"].b])
                                        tt(DVE, bon[:, oc, :], bon[:, oc, :], pt["u2"][:], ALU.add, [bon.b, pt["u2"].b], [bon.b])
                                    tt(DVE, pt["u1"][:], pt["kk"][:], pt["af"][:], ALU.mult, [pt["kk"].b, pt["af"].b], [pt["u1"].b])
                                    yield
                                    for hh in range(2):
                                        hs_ = half(hh)
                                        v3 = lambda t_: t_[hs_, :].rearrange("p (c t) -> p c t", t=64)
                                        bdv = lambda n_: BD[n_][hs_, :, oc, hh * 64:(hh + 1) * 64]
                                        stt(bdv("aT"), v3(pt["kk"]), -1.0, v3(pt["Eex"]), ALU.mult, ALU.mult, [pt["kk"].b, pt["Eex"].b], [BD["aT"].b])
                                        tt(DVE, bdv("bT"), v3(pt["u1"]), v3(pt["Em"]), ALU.mult, [pt["u1"].b, pt["Em"].b], [BD["bT"].b])
                                        tt(DVE, bdv("kT"), v3(pt["kd"]), v3(pt["Em"]), ALU.mult, [pt["kd"].b, pt["Em"].b], [BD["kT"].b])
                                        tt(DVE, bdv("rT"), v3(pt["rf"]), v3(pt["Ep"]), ALU.mult, [pt["rf"].b, pt["Ep"].b], [BD["rT"].b])
                                        cp(ACT, bdv("vT"), v3(pt["vf"]), [pt["vf"].b], [BD["vT"].b])
                                    yield

                                for o2 in range(2):
                                    c2 = slice(pg * 512 + o2 * 256, pg * 512 + o2 * 256 + 256)
                                    Wr, Wk, Wv = wbnext(), wbnext(), wbnext()
                                    dma(POOL, Wr[:], wr_v[:, :, c2], [], [Wr.b])
                                    dma(POOL, Wk[:], wk_v[:, :, c2], [], [Wk.b])
                                    dma(POOL, Wv[:], wv_v[:, :, c2], [], [Wv.b])
                                    interleave([prep(o2 * 2 + j_, Wr, Wk, Wv, pts[j_]) for j_ in range(2)])
                                corder = list(range(4)) if e == 0 else list(range(3, -1, -1))
                                bd4 = lambda n_, c_: (BD[n_][:, c_], BD[n_].b)

                                def cst(c_, n_):
                                    t_ = CS[c_ % NSET][alias.get(n_, n_)]
                                    return (t_, t_.b)
                                idb = (identb, identb.b)
                                def par(c_):
                                    for src, dst in (("aT", "A_tm"), ("bT", "B_tm"), ("kT", "K_tm"), ("vT", "V_tm")):
                                        pb = pnext()
                                        mm4(pb, bd4(src, c_), idb)
                                        d_ = cst(c_, dst)
                                        cp(ACT, d_[0][:], pv(pb), [pb.b], [d_[1]])
                                    for (l_, r_, dst, msk) in (("bT", "aT", "P0", mBef), ("aT", "bT", "Q0", mBefT), ("kT", "aT", "N_ak", mBef),
                                                               ("bT", "rT", "N_rb", mInc), ("kT", "rT", "N_rk", mInc)):
                                        pb = pnext()
                                        mm4(pb, bd4(l_, c_), bd4(r_, c_))
                                        d_ = cst(c_, dst)
                                        tt(DVE, d_[0][:], pv(pb), msk[:], ALU.mult, [pb.b, msk.b], [d_[1]])
                                    yield
                                    X_ = [cst(c_, "Xb"), cst(c_, "Xc")]
                                    P0_ = cst(c_, "P0")
                                    tt(DVE, X_[0][0][:], P0_[0][:], identb[:], ALU.add, [P0_[1], identb.b], [X_[0][1]])
                                    yield
                                    for lev in range(5):
                                        pi, po_ = lev % 2, (lev + 1) % 2
                                        Pi, Qi = cst(c_, f"P{pi}"), cst(c_, f"Q{pi}")
                                        Po, Qo = cst(c_, f"P{po_}"), cst(c_, f"Q{po_}")
                                        pb = pnext()
                                        mm4(pb, Pi, Qi)
                                        cp(DVE if lev % 2 == 0 else ACT, Qo[0][:], pv(pb), [pb.b], [Qo[1]])
                                        if lev < 4:
                                            pb = pnext()
                                            mm4(pb, Qi, Pi)
                                            cp(ACT, Po[0][:], pv(pb), [pb.b], [Po[1]])
                                        yield
                                        Xi, Xo = X_[lev % 2], X_[(lev + 1) % 2]
                                        pb = pnext()
                                        mm4multi(pb, [(Qo, Xi, None, None), (idb, Xi, None, None)])
                                        cp(ACT, Xo[0][:], pv(pb), [pb.b], [Xo[1]])
                                        yield
                                    TT_ = X_[1]
                                    pb = pnext()
                                    mm4(pb, cst(c_, "N_ak"), cst(c_, "V_tm"))
                                    cp(ACT, cst(c_, "X1")[0][:], pv(pb), [pb.b], [cst(c_, "X1")[1]])
                                    pb = pnext()
                                    mm4(pb, TT_, cst(c_, "A_tm"))
                                    cp(DVE, cst(c_, "Ah")[0][:], pv(pb), [pb.b], [cst(c_, "Ah")[1]])
                                    yield
                                    pb = pnext()
                                    mm4(pb, TT_, cst(c_, "X1"))
                                    cp(ACT, cst(c_, "Ul")[0][:], pv(pb), [pb.b], [cst(c_, "Ul")[1]])
                                    pb = pnext()
                                    mm4(pb, cst(c_, "Ah"), cst(c_, "N_rb"))
                                    tt(DVE, cst(c_, "RhT")[0][:], pv(pb), BD["rT"][:, c_], ALU.add, [pb.b, BD["rT"].b], [cst(c_, "RhT")[1]])
                                    pb = pnext()
                                    mm4(pb, cst(c_, "Ah"), cst(c_, "B_tm"))
                                    tt(DVE, cst(c_, "GpT")[0][:], pv(pb), identb[:], ALU.add, [pb.b, identb.b], [cst(c_, "GpT")[1]])
                                    yield

                                def seq(c_):
                                    gcb = gC[:, :, c_:c_ + 1].to_broadcast([128, 4, 128])
                                    Mq = [Mb[:, pg * 4 + qi, :] for qi in range(4)]
                                    pb = pnext()
                                    mm4multi(pb, [(cst(c_, "Ul"), cst(c_, "N_rb"), None, None), (cst(c_, "V_tm"), cst(c_, "N_rk"), None, None),
                                                  ((None, Mb.b), cst(c_, "RhT"), Mq, None)])
                                    for hh in range(2):
                                        hs_ = half(hh)
                                        o_dst = o_fm[hs_, :, c_ * 64:(c_ + 1) * 64]
                                        o_src = pv(pb)[hs_, :, hh * 64:(hh + 1) * 64]
                                        if e == 0:
                                            cp(DVE, o_dst, o_src, [pb.b], [o_fm.b])
                                        else:
                                            tt(DVE, o_dst, o_src, o_dst, ALU.add, [pb.b, o_fm.b], [o_fm.b])
                                    pb = pnext()
                                    mm4multi(pb, [(cst(c_, "B_tm"), cst(c_, "Ul"), None, None), (cst(c_, "K_tm"), cst(c_, "V_tm"), None, None),
                                                  (cst(c_, "GpT"), (None, Mb.b), None, Mq)])
                                    Mf4 = Mf[:, pg * 4:pg * 4 + 4, :]
                                    tt(DVE, Mf4, pv(pb), gcb, ALU.mult, [pb.b, gC.b], [Mf.b])
                                    cp(ACT, Mb[:, pg * 4:pg * 4 + 4, :], Mf4, [Mf.b], [Mb.b])
                                for i_ in range(0, 4, NSET):
                                    interleave([par(c_) for c_ in corder[i_:i_ + NSET]])
                                    for c_ in corder[i_:i_ + NSET]:
                                        seq(c_)
                                if e == 0:
                                    dma(SP, osc[:, pg * 4:pg * 4 + 4, g0:g0 + NB], o_fm[:], [o_fm.b], [osc_b], chan=o_fm.b)
                                    dma(SP, bsc[:, pg * 4:pg * 4 + 4, g0:g0 + NB], bon[:], [bon.b], [bsc_b], chan=bon.b)
                                else:
                                    if _dbg:
                                        dma(SP, osc[:, pg * 4:pg * 4 + 4, g0:g0 + NB], o_fm[:], [o_fm.b], [osc_b], chan=o_fm.b)
                                        dma(SP, bsc[:, pg * 4:pg * 4 + 4, g0:g0 + NB], bon[:], [bon.b], [bsc_b], chan=bon.b)
                                    def gn(oc, pt):
                                        kcg = pg * 4 + oc
                                        osl5 = slice(oc * 128, (oc + 1) * 128)
                                        pb = pnext()
                                        mm(pb[:, 0:NB], bones[:], o_fm[:, oc, :], True, True, [bones.b, o_fm.b], [pb.b])
                                        pg_ = pnext()
                                        for j2 in range(2):
                                            mm(pg_[:, 0:NB], g2t[:, j2, osl5], Lg[:, j2, :], j2 == 0, j2 == 1, [g2t.b, Lg.b], [pg_.b])
                                        stt(pt["u1"][:], pb[:, 0:NB], -1.0 / 64, o_fm[:, oc, :], ALU.mult, ALU.add, [pb.b, o_fm.b], [pt["u1"].b])
                                        tt(DVE, pt["u2"][:], pt["u1"][:], pt["u1"][:], ALU.mult, [pt["u1"].b], [pt["u2"].b])
                                        yield
                                        pb = pnext()
                                        mm(pb[:, 0:NB], bones[:], pt["u2"][:], True, True, [bones.b, pt["u2"].b], [pb.b])
                                        act(pt["u3"][:], pb[:, 0:NB], AF.Sqrt, [pb.b, gneps.b], [pt["u3"].b], scale=1.0 / 64, bias=gneps[:, 0:1])
                                        yield
                                        P.op(DVE, lambda e_: e_.reciprocal(out=pt["u3"][:], in_=pt["u3"][:]), reads=[pt["u3"].b], writes=[pt["u3"].b])
                                        stt(pt["u1"][:], pt["u1"][:], V("ln_w", kcg), pt["u3"][:], ALU.mult, ALU.mult, [pt["u1"].b, vecs.b, pt["u3"].b], [pt["u1"].b])
                                        stt(pt["u1"][:], pt["u1"][:], V("ln_b", kcg), bon[:, oc, :], ALU.add, ALU.add, [pt["u1"].b, vecs.b, bon.b], [pt["u1"].b])
                                        tt(DVE, yb[:, kcg, :], pg_[:, 0:NB], pt["u1"][:], ALU.mult, [pg_.b, pt["u1"].b], [yb.b])
                                        yield
                                    for o2 in range(2):
                                        interleave([gn(o2 * 2 + j_, pts[j_]) for j_ in range(2)])
                            if e == 1:
                                for d8 in range(8):
                                    Wo = wbnext()
                                    dma(POOL, Wo[:], wo_v[:, :, d8 * 256:(d8 + 1) * 256], [], [Wo.b])
                                    for dd in range(2):
                                        dc = d8 * 2 + dd
                                        xr_ = xres[dc % 2]
                                        dma(SP, xr_[:], scr_v[:, dc, g0:g0 + NB], scr_b, [xr_.b])
                                        pb = pnext()
                                        for kc in range(KC):
                                            mm(pb[:, 0:NB], Wo[:, kc, dd * 128:(dd + 1) * 128], yb[:, kc, :], kc == 0, kc == KC - 1, [Wo.b, yb.b], [pb.b])
                                        stt(xr_[:], pb[:, 0:NB], mod[:, 32 + dc, cc:cc + 1], xr_[:], ALU.mult, ALU.add, [pb.b, mod.b, xr_.b], [xr_.b])
                                        dma(SP, scr_v2[1 - scur][:, dc, g0:g0 + NB], xr_[:], [xr_.b], scr_b2[1 - scur], chan=xr_.b)
                            if last_of_seq and cc == 1:
                                pr = (q0 - 2048) // 256
                                for p_ in range(KC):
                                    pb = pnext()
                                    tr(pb[:, 0:128], Mf[:, p_, :], [Mf.b], [pb.b])
                                    so_ = so[p_ % 2]
                                    cp(ACT, so_[:], pb[:, 0:128], [pb.b], [so_.b])
                                    for hh in range(2):
                                        dma(SP, st_out[pr, e, p_ * 128 + hh * 64:p_ * 128 + hh * 64 + 64, :], so_[half(hh), hh * 64:(hh + 1) * 64],
                                            [so_.b], [st_b], chan=so_.b)
                scur = 1 - scur
                scr_v, scr_b = scr_v2[scur], scr_b2[scur]
                P.barrier()

            if mixers and kind == 2:
                with ExitStack() as ph:
                    kT_all = sb("kT_all", [128, 4, NTOK], BF16, stack=ph)
                    V_all = sb("V_all", [128, 20, 512], BF16, stack=ph)
                    kTc = sb("kTc", [128, 4, 512], BF16, stack=ph)
                    V_c = sb("V_c", [128, 4, 512], BF16, stack=ph)
                    masks = sb("amasks", [128, 6, 512], BF16, stack=ph)
                    rotT = sb("rotT", [128, 128], F32, stack=ph)
                    esink = sb("esink", [128, 16], F32, stack=ph)
                    onesb = sb("onesb", [128, 128], BF16, stack=ph)
                    xs = sb("axs", [128, KC, 512], F32, stack=ph)
                    hT = sb("ahT", [128, KC, 512], BF16, stack=ph)
                    rr = sb("arr", [128, 512], F32, stack=ph)
                    sqt = [sb(f"asq{i}", [128, 512], F32, stack=ph) for i in range(2)]
                    tmpn = [sb(f"atmp{i}", [128, 512], F32, stack=ph) for i in range(2)]
                    ropeC = sb("ropeC", [128, 512], F32, stack=ph)
                    ropeS = sb("ropeS", [128, 512], F32, stack=ph)
                    qsq = sb("qsq", [128, 512], F32, stack=ph)
                    qr = sb("qr", [128, 512], F32, stack=ph)
                    qn = sb("qn", [128, 512], F32, stack=ph)
                    t1 = sb("at1", [128, 512], F32, stack=ph)
                    t2 = sb("at2", [128, 512], F32, stack=ph)
                    epsh = epsb
                    ck_b, cv_b = Buf("ck_out"), Buf("cv_out")

                    dma(SP, masks[:], amask_d, [], [masks.b])
                    dma(SP, rotT[:], rotT_d, [], [rotT.b])
                    memset(DVE, onesb[:], 1.0, [onesb.b])
                    act(esink[:], V("sink"), AF.Exp, [vecs.b], [esink.b])
                    dma(POOL, V_c[:], cache_v.rearrange("(t p) n -> p t n", p=128), [], [V_c.b])

                    def prenorm_seg(si):
                        s0, sl, cc = SEGS[si]
                        dma(SP, xs[:], scr_v[:, :, s0:s0 + sl], [scr_b[si]], [xs.b])
                        rstd_cols(lambda kc: xs[:, kc, :], 512, rr[:], rr.b, xs.b, sqt, psum[1])
                        for kc in range(KC):
                            tm = tmpn[kc % 2]
                            stt(tm[:], xs[:, kc, :], gs1[:, kc, cc:cc + 1], rr[:], ALU.mult, ALU.mult, [xs.b, gs1.b, rr.b], [tm.b])
                            act(hT[:, kc, :], tm[:], AF.Identity, [tm.b, mod.b], [hT.b], bias=mod[:, kc, cc:cc + 1])
                        if cc == 0:
                            dma(SP, ropeC[:], ropeC_d[:, s0:s0 + 512], [], [ropeC.b])
                            dma(SP, ropeS[:], ropeS_d[:, s0:s0 + 512], [], [ropeS.b])

                    def qk_head(pb, gname, rope, out_ap, out_b):
                        act(qsq[:], pb[:, 0:512], AF.Square, [pb.b], [qsq.b])
                        mm(psum[1][:, 0:512], ones[:], qsq[:], True, True, [ones.b, qsq.b], [psum[1].b])
                        act(qr[:], psum[1][:, 0:512], AF.Sqrt, [psum[1].b, epsb.b], [qr.b], scale=1.0 / 128, bias=epsb[:, 0:1])
                        P.op(DVE, lambda e: e.reciprocal(out=qr[:], in_=qr[:]), reads=[qr.b], writes=[qr.b])
                        stt(qn[:], pb[:, 0:512], V(gname), qr[:], ALU.mult, ALU.mult, [pb.b, vecs.b, qr.b], [qn.b])
                        if rope:
                            mm(psum[2][:, 0:512], rotT[:], qn[:], True, True, [rotT.b, qn.b], [psum[2].b])
                            tt(DVE, t1[:], qn[:], ropeC[:], ALU.mult, [qn.b, ropeC.b], [t1.b])
                            tt(DVE, t2[:], psum[2][:, 0:512], ropeS[:], ALU.mult, [psum[2].b, ropeS.b], [t2.b])
                            tt(DVE, out_ap, t1[:], t2[:], ALU.add, [t1.b, t2.b], [out_b])
                        else:
                            cp(DVE, out_ap, qn[:], [qn.b], [out_b])

                    with ExitStack() as phA:
                        wk = sb("awk", [128, KC, 512], BF16, stack=phA)
                        wv_ = sb("awv", [128, KC, 512], BF16, stack=phA)
                        ckt = sb("ckt", [128, 4, 512], F32, stack=phA)
                        kout = [sb(f"kout{i}", [128, 512], F32, stack=phA) for i in range(2)]
                        vout = [sb(f"vout{i}", [128, 512], F32, stack=phA) for i in range(2)]
                        qkv_v = w_qkv.rearrange("(c p) n -> p c n", p=128)
                        dma(POOL, wk[:], qkv_v[:, :, 2048:2560], [], [wk.b])
                        dma(POOL, wv_[:], qkv_v[:, :, 2560:3072], [], [wv_.b])
                        dma(SP, ckt[:], cache_k.rearrange("(t p) n -> p t n", p=128), [], [ckt.b])
                        import os as _os
                        ASTG = int(_os.environ.get("ATT_STAGE", "9"))
                        for kh in range(4 if ASTG >= 2 else 0):
                            pb = psum[4 + kh]
                            for t4 in range(4):
                                tr(pb[:, t4 * 128:(t4 + 1) * 128], ckt[:, t4, kh * 128:(kh + 1) * 128], [ckt.b], [pb.b])
                            cp(ACT, kTc[:, kh, :], pb[:, 0:512], [pb.b], [kTc.b])
                        for si, (s0, sl, cc) in enumerate(SEGS[:(0 if ASTG < 3 else (4 if ASTG == 3 else 5))]):
                            prenorm_seg(si)
                            for kh in range(4):
                                pb = psum[0]
                                for kc in range(KC):
                                    mm(pb[:, 0:512], wk[:, kc, kh * 128:(kh + 1) * 128], hT[:, kc, :], kc == 0, kc == KC - 1, [wk.b, hT.b], [pb.b])
                                qk_head(pb, "kn", cc == 0, kT_all[:, kh, s0:s0 + 512], kT_all.b)
                                if cc == 1 and ASTG != 5:
                                    for t4 in range(4):
                                        tr(psum[4 + t4][:, kh * 128:(kh + 1) * 128], qn[:, t4 * 128:(t4 + 1) * 128], [qn.b], [psum[4 + t4].b])
                            if cc == 1 and ASTG != 5:
                                for t4 in range(4):
                                    ko = kout[t4 % 2]
                                    cp(ACT, ko[:], psum[4 + t4][:, 0:512], [psum[4 + t4].b], [ko.b])
                                    dma(SP, ck_out[t4 // 2, (t4 % 2) * 128:(t4 % 2) * 128 + 128, :], ko[:], [ko.b], [ck_b], chan=ko.b)
                            for t4 in range(4):
                                t_ = (s0 // 128) + t4
                                pb = psum[3]
                                for kc in range(KC):
                                    mm(pb[:, 0:512], hT[:, kc, t4 * 128:(t4 + 1) * 128], wv_[:, kc, :], kc == 0, kc == KC - 1, [hT.b, wv_.b], [pb.b])
                                cp(ACT, V_all[:, t_, :], pb[:, 0:512], [pb.b], [V_all.b])
                                if cc == 1 and ASTG != 6:
                                    vo = vout[t4 % 2]
                                    cp(DVE, vo[:], pb[:, 0:512], [pb.b], [vo.b])
                                    dma(SP, cv_out[t4 // 2, (t4 % 2) * 128:(t4 % 2) * 128 + 128, :], vo[:], [vo.b], [cv_b], chan=vo.b)
                    P.barrier()

                    with ExitStack() as phB:
                        wts = [sb(f"abw{i}", [128, KC, 512], BF16, stack=phB) for i in range(2)]
                        oT = sb("aoT", [128, 16, 512], BF16, stack=phB)
                        qT = [sb(f"aqT{i}", [128, 512], BF16, stack=phB) for i in range(2)]
                        Et = [sb(f"aE{i}", [128, 512], BF16, stack=phB) for i in range(3)]
                        rden = sb("rden", [128, 512], F32, stack=phB)
                        qkv_v = w_qkv.rearrange("(c p) n -> p c n", p=128)
                        wo_v = w_ao.rearrange("(c p) n -> p c n", p=128)
                        wc = [0]
                        ec = [0]

                        def wnext():
                            w = wts[wc[0] % 2]
                            wc[0] += 1
                            return w
                        import os as _os
                        for si, (s0, sl, cc) in enumerate(SEGS if _os.environ.get('ATT_PASSB', '1') == '1' else []):
                            prenorm_seg(si)
                            qw = [None]

                            def qproj(h):
                                if h % 4 == 0:
                                    qw[0] = wnext()
                                    dma(POOL, qw[0][:], qkv_v[:, :, (h // 4) * 512:(h // 4 + 1) * 512], [], [qw[0].b])
                                w = qw[0]
                                pb = psum[0]
                                for kc in range(KC):
                                    mm(pb[:, 0:512], w[:, kc, (h % 4) * 128:(h % 4 + 1) * 128], hT[:, kc, :], kc == 0, kc == KC - 1, [w.b, hT.b], [pb.b])
                                q = qT[h % 2]
                                qk_head(pb, "qn", cc == 0, q[:], q.b)

                            def attend(h):
                                kv = h // 4
                                q = qT[h % 2]
                                pden, po = psum[5], psum[6]
                                if cc == 0:
                                    groups = [(0, 512, [("l", kb) for kb in range(max(0, s0 // 128 - 1), min(15, s0 // 128 + 4) + 1)] + [("c", cb) for cb in range(4)])]
                                else:
                                    groups = [(pr * 256, 256, [("l", 16 + 2 * pr), ("l", 17 + 2 * pr)]) for pr in range(2)]
                                for (c0, n, blocks) in groups:
                                    def operands(typ, kb):
                                        if typ == "l":
                                            return (kT_all[:, kv, kb * 128:(kb + 1) * 128], kT_all.b, V_all[:, kb, kv * 128:(kv + 1) * 128], V_all.b)
                                        return (kTc[:, kv, kb * 128:(kb + 1) * 128], kTc.b, V_c[:, kb, kv * 128:(kv + 1) * 128], V_c.b)

                                    def scores(bi):
                                        typ, kb = blocks[bi]
                                        kk_ap, kk_b, _, _ = operands(typ, kb)
                                        pS = psum[3 + (ec[0] + bi) % 2]
                                        mm(pS[:, 0:n], kk_ap, q[:, c0:c0 + n], True, True, [kk_b, q.b], [pS.b])
                                    scores(0)
                                    for bi, (typ, kb) in enumerate(blocks):
                                        if bi + 1 < len(blocks):
                                            scores(bi + 1)
                                        pS = psum[3 + (ec[0] + bi) % 2]
                                        E = Et[(ec[0] + bi) % 3]
                                        _, _, vv_ap, vv_b = operands(typ, kb)
                                        act(E[:, 0:n], pS[:, 0:n], AF.Exp, [pS.b], [E.b], scale=float(128 ** -0.5))
                                        if typ == "l" and cc == 0:
                                            r = kb - s0 // 128 + 1
                                            tt(DVE, E[:, 0:n], E[:, 0:n], masks[:, r, 0:n], ALU.mult, [E.b, masks.b], [E.b])
                                        first, last = bi == 0, bi == len(blocks) - 1
                                        mm(pden[:, c0:c0 + n], onesb[:], E[:, 0:n], first, last, [onesb.b, E.b], [pden.b])
                                        mm(po[:, c0:c0 + n], vv_ap, E[:, 0:n], first, last, [vv_b, E.b], [po.b])
                                    ec[0] += len(blocks)
                                ts(DVE, rden[:], pden[:, 0:512], esink[:, h:h + 1], ALU.add, [pden.b, esink.b], [rden.b])
                                P.op(DVE, lambda e: e.reciprocal(out=rden[:], in_=rden[:]), reads=[rden.b], writes=[rden.b])
                                tt(DVE, oT[:, h, :], po[:, 0:512], rden[:], ALU.mult, [po.b, rden.b], [oT.b])

                            qproj(0)
                            for h in range(16):
                                if h + 1 < 16:
                                    qproj(h + 1)
                                attend(h)
                            for d4 in range(4):
                                w = wnext()
                                dma(POOL, w[:], wo_v[:, :, d4 * 512:(d4 + 1) * 512], [], [w.b])
                                for dd in range(4):
                                    dc = d4 * 4 + dd
                                    pb = psum[7] if dd % 2 == 0 else psum[0]
                                    for h in range(16):
                                        mm(pb[:, 0:512], w[:, h, dd * 128:(dd + 1) * 128], oT[:, h, :], h == 0, h == 15, [w.b, oT.b], [pb.b])
                                    stt(xs[:, dc, :], pb[:, 0:512], mod[:, 32 + dc, cc:cc + 1], xs[:, dc, :], ALU.mult, ALU.add, [pb.b, mod.b, xs.b], [xs.b])
                            dma(SP, scr_v2[1 - scur][:, :, s0:s0 + 512], xs[:], [xs.b], [scr_b2[1 - scur][si]], chan=xs.b)
                scur = 1 - scur
                scr_v, scr_b = scr_v2[scur], scr_b2[scur]
                P.barrier()

            if ffn:
                with ExitStack() as ph:
                    XW = 520
                    xs = sb("fxs", [128, KC, XW], F32, stack=ph)
                    hT = sb("fhT", [128, KC, XW], BF16, stack=ph)
                    actT = sb("actT", [128, FC, 512], BF16, stack=ph)
                    rr = sb("frr", [128, XW], F32, stack=ph)
                    sqt = [sb(f"fsq{i}", [128, 512], F32, stack=ph) for i in range(2)]
                    tmpn = [sb(f"ftmp{i}", [128, XW], F32, stack=ph) for i in range(2)]
                    stg = [[sb(f"stg{i}{j}", [128, XW], F32, stack=ph) for j in range(2)] for i in range(2)]
                    acc = [[sb(f"acc{i}{j}", [128, 512], F32, stack=ph) for j in range(2)] for i in range(2)]
                    sg = [sb(f"sg{i}", [128, 512], F32, stack=ph) for i in range(2)]
                    WSZ = 11264
                    NWB = 3
                    wts = [sb(f"fw{i}", [128, WSZ], BF16, stack=ph) for i in range(NWB)]
                    up_v = ffn_up[l].rearrange("(c p) (g f) -> p c g f", p=128, g=2)
                    dn_v = ffn_down[l].rearrange("(c p) n -> p c n", p=128)
                    for i in range(2):
                        for j in range(2):
                            memset(DVE, stg[i][j][:], 0.0, [stg[i][j].b])
                    wctr = [0]

                    def wnext():
                        w = wts[wctr[0] % NWB]
                        wctr[0] += 1
                        return w

                    for si, (s0, sl, cc) in enumerate(SEGS):
                        sample = cc == 0
                        if sample:
                            runs = [(1, 512)]
                            lo_t = max(s0 - 1, 0)
                            hi_t = min(s0 + 513, 2048)
                            c_lo = 1 - (s0 - lo_t)
                            if s0 == 0:
                                memset(DVE, xs[:, :, 0:1], 0.0, [xs.b])
                            if s0 + 512 == 2048:
                                memset(DVE, xs[:, :, 513:514], 0.0, [xs.b])
                            dma(SP, xs[:, :, c_lo:c_lo + (hi_t - lo_t)], scr_v[:, :, lo_t:hi_t], [scr_b[si]] + ([scr_b[si - 1]] if si > 0 else []) + ([scr_b[si + 1]] if si < 3 else []), [xs.b])
                            mainv = lambda t_, kc: t_[:, kc, 1:513]
                            halov = lambda t_, kc: t_[:, kc, 0:514:513]
                            main2 = lambda t_: t_[:, 1:513]
                            halo2 = lambda t_: t_[:, 0:514:513]
                        else:
                            runs = [(1, 256), (259, 256)]
                            for r_ in range(2):
                                dma(SP, xs[:, :, 1 + 258 * r_:257 + 258 * r_], scr_v[:, :, s0 + 256 * r_:s0 + 256 * r_ + 256], [scr_b[si]], [xs.b])
                            mainv = lambda t_, kc: t_[:, kc, 1:517].rearrange("p (r c) -> p r c", c=258)[:, :, 0:256]
                            main2 = lambda t_: t_[:, 1:517].rearrange("p (r c) -> p r c", c=258)[:, :, 0:256]
                            memset(DVE, hT[:], 0.0, [hT.b])
                            for i in range(2):
                                for j in range(2):
                                    memset(DVE, stg[i][j][:, 0:1], 0.0, [stg[i][j].b])
                                    memset(DVE, stg[i][j][:, 257:259], 0.0, [stg[i][j].b])
                                    memset(DVE, stg[i][j][:, 515:516], 0.0, [stg[i][j].b])
                        ps2 = lambda pb: pb[:, 0:512].rearrange("p (r c) -> p r c", c=256) if not sample else pb[:, 0:512]
                        pm, phb = psum[6], psum[7]
                        for kc in range(KC):
                            q = sqt[kc % 2]
                            q2 = q[:, 0:512].rearrange("p (r c) -> p r c", c=256) if not sample else q[:, 0:512]
                            act(q2, mainv(xs, kc), AF.Square, [xs.b], [q.b])
                            mm(pm[:, 0:512], ones[:], q[:, 0:512], kc == 0, kc == KC - 1, [ones.b, q.b], [pm.b])
                        act(main2(rr), ps2(pm), AF.Sqrt, [pm.b, epsb.b], [rr.b], scale=1.0 / D, bias=epsb[:, 0:1])
                        if sample:
                            for kc in range(KC):
                                q = sqt[kc % 2]
                                act(q[:, 0:2], halov(xs, kc), AF.Square, [xs.b], [q.b])
                                mm(phb[:, 0:2], ones[:], q[:, 0:2], kc == 0, kc == KC - 1, [ones.b, q.b], [phb.b])
                            act(halo2(rr), phb[:, 0:2], AF.Sqrt, [phb.b, epsb.b], [rr.b], scale=1.0 / D, bias=epsb[:, 0:1])
                            P.op(DVE, lambda e: e.reciprocal(out=rr[:, 0:514], in_=rr[:, 0:514]), reads=[rr.b], writes=[rr.b])
                        else:
                            P.op(DVE, lambda e: e.reciprocal(out=main2(rr), in_=main2(rr)), reads=[rr.b], writes=[rr.b])
                        for kc in range(KC):
                            tm = tmpn[kc % 2]
                            if sample:
                                stt(tm[:, 0:514], xs[:, kc, 0:514], gs2[:, kc, cc:cc + 1], rr[:, 0:514], ALU.mult, ALU.mult,
                                    [xs.b, gs2.b, rr.b], [tm.b])
                                act(hT[:, kc, 0:514], tm[:, 0:514], AF.Identity, [tm.b, mod.b], [hT.b], bias=mod[:, 48 + kc, cc:cc + 1])
                            else:
                                stt(main2(tm), mainv(xs, kc), gs2[:, kc, cc:cc + 1], main2(rr), ALU.mult, ALU.mult,
                                    [xs.b, gs2.b, rr.b], [tm.b])
                                act(mainv(hT, kc), main2(tm), AF.Identity, [tm.b, mod.b], [hT.b], bias=mod[:, 48 + kc, cc:cc + 1])
                        if sample:
                            if s0 == 0:
                                memset(DVE, hT[:, :, 0:1], 0.0, [hT.b])
                            if s0 + 512 == 2048:
                                memset(DVE, hT[:, :, 513:514], 0.0, [hT.b])
                        for st2 in range(FC // 2):
                            w = wnext()
                            wv = w[:, 0:KC * 2 * 256].rearrange("p (c g f) -> p c g f", c=KC, g=2)
                            for g_ in range(2):
                                dma(POOL, wv[:, :, g_, :], up_v[:, :, g_, st2 * 256:(st2 + 1) * 256], [], [w.b])
                            for f2 in range(2):
                                fc = st2 * 2 + f2
                                bset = fc % 2
                                pg, pv, phh = psum[bset * 3], psum[bset * 3 + 1], psum[bset * 3 + 2]
                                for g, pb in ((0, pg), (1, pv)):
                                    for kc in range(KC):
                                        mm(ps2(pb), wv[:, kc, g, f2 * 128:(f2 + 1) * 128], mainv(hT, kc), kc == 0, kc == KC - 1,
                                           [w.b, hT.b], [pb.b])
                                    if sample:
                                        for kc in range(KC):
                                            mm(phh[:, 2 * g:2 * g + 2], wv[:, kc, g, f2 * 128:(f2 + 1) * 128], halov(hT, kc), kc == 0, kc == KC - 1,
                                               [w.b, hT.b], [phh.b])
                                for g, pb in ((0, pg), (1, pv)):
                                    sgt = stg[bset][g]
                                    ag = acc[bset][g]
                                    cp(ACT, main2(sgt), ps2(pb), [pb.b], [sgt.b])
                                    if sample:
                                        cp(ACT, halo2(sgt), phh[:, 2 * g:2 * g + 2], [phh.b], [sgt.b])
                                    fcol = g * FC + fc
                                    for ri, (c0, n) in enumerate(runs):
                                        a_ = ag[:, ri * 256:ri * 256 + n]
                                        ts(DVE, a_, sgt[:, c0:c0 + n], V(f"cw{l}_1", fcol), ALU.mult, [sgt.b, vecs.b], [ag.b],
                                           s2=V(f"cb{l}", fcol), op1=ALU.add)
                                        stt(a_, sgt[:, c0 - 1:c0 - 1 + n], V(f"cw{l}_0", fcol), a_, ALU.mult, ALU.add, [sgt.b, vecs.b, ag.b], [ag.b])
                                        stt(a_, sgt[:, c0 + 1:c0 + 1 + n], V(f"cw{l}_2", fcol), a_, ALU.mult, ALU.add, [sgt.b, vecs.b, ag.b], [ag.b])
                                s_ = sg[bset]
                                act(s_[:], acc[bset][0][:], AF.Silu, [acc[bset][0].b], [s_.b])
                                tt(DVE, actT[:, fc, :], s_[:], acc[bset][1][:], ALU.mult, [s_.b, acc[bset][1].b], [actT.b])
                        for d2 in range(KC // 2):
                            w = wnext()
                            wv = w[:, 0:FC * 256].rearrange("p (c n) -> p c n", c=FC)
                            dma(POOL, wv, dn_v[:, :, d2 * 256:(d2 + 1) * 256], [], [w.b])
                            for dd in range(2):
                                dc = d2 * 2 + dd
                                pb = psum[dc % 4]
                                for fc in range(FC):
                                    mm(pb[:, 0:512], wv[:, fc, dd * 128:(dd + 1) * 128], actT[:, fc, :], fc == 0, fc == FC - 1,
                                       [w.b, actT.b], [pb.b])
                                stt(mainv(xs, dc), ps2(pb), mod[:, 80 + dc, cc:cc + 1], mainv(xs, dc), ALU.mult, ALU.add,
                                    [pb.b, mod.b, xs.b], [xs.b])
                        if sample:
                            dma(SP, scr_v2[1 - scur][:, :, s0:s0 + 512], xs[:, :, 1:513], [xs.b], [scr_b2[1 - scur][si]], chan=xs.b)
                        else:
                            for r_ in range(2):
                                dma(SP, scr_v2[1 - scur][:, :, s0 + 256 * r_:s0 + 256 * r_ + 256], xs[:, :, 1 + 258 * r_:257 + 258 * r_],
                                    [xs.b], [scr_b2[1 - scur][si]], chan=xs.b)
                scur = 1 - scur
                scr_v, scr_b = scr_v2[scur], scr_b2[scur]
                P.barrier()

        with ExitStack() as ph:
            xt = [sb(f"oxt{i}", [128, D], F32, stack=ph) for i in range(2)]
            xst = [sb(f"oxst{i}", [128, KC, 128], F32, stack=ph) for i in range(2)]
            for t_ in range(NTOK // 128):
                a, s_ = xt[t_ % 2], xst[t_ % 2]
                dma(SP, s_[:], scr_v[:, :, t_ * 128:(t_ + 1) * 128], [scr_b[t_ // 4]], [s_.b])
                for g in range(4):
                    pb = psum[(t_ * 4 + g) % 8]
                    for j in range(4):
                        kc = g * 4 + j
                        tr(pb[:, j * 128:(j + 1) * 128], s_[:, kc, :], [s_.b], [pb.b])
                    cp(ACT if g % 2 == 0 else DVE, a[:, g * 512:(g + 1) * 512], pb[:, 0:512], [pb.b], [a.b])
                dma(SP, y[t_ * 128:(t_ + 1) * 128, :], a[:], [a.b], [y_b], chan=a.b)
        P.barrier()
        P.emit()
    return nc


def attn_consts():
    import ml_dtypes
    i = np.arange(128)[:, None, None]
    r = np.arange(6)[None, :, None] - 1
    j = np.arange(512)[None, None, :]
    amask = (np.abs(j - (128 * r + i)) <= 128).astype(np.float32)
    rot = np.zeros((128, 128), np.float32)
    for a in range(2):
        for f in range(32):
            m0, m1 = a * 64 + f, a * 64 + 32 + f
            rot[m1, m0] = -1.0
            rot[m0, m1] = 1.0
    t = np.arange(2048)
    row = (t // 64).astype(np.float32)
    col = (t % 64).astype(np.float32)
    inv = (np.float32(10000.0) ** (-np.arange(32, dtype=np.float32) / np.float32(32))).astype(np.float32)
    ang = np.zeros((128, 2048), np.float32)
    for a, pos in enumerate((row, col)):
        for hlf in range(2):
            ang[a * 64 + hlf * 32:a * 64 + hlf * 32 + 32, :] = inv[:, None] * pos[None, :]
    return {"amask": amask.astype(ml_dtypes.bfloat16), "rotT": rot, "ropeC": np.cos(ang).astype(np.float32), "ropeS": np.sin(ang).astype(np.float32)}


def rwkv_consts():
    import ml_dtypes
    i = np.arange(64)[:, None]
    j = np.arange(64)[None, :]
    conds = [i < j, i > j, i <= j, i >= j]
    rmask = np.zeros((4, 128, 4, 128), np.float32)
    for m, c in enumerate(conds):
        for h in range(2):
            rmask[m, h * 64:(h + 1) * 64, :, h * 64:(h + 1) * 64] = c[:, None, :]
    identb = np.zeros((128, 4, 128), np.float32)
    identb[:] = np.eye(128, dtype=np.float32)[:, None, :]
    bones = np.zeros((128, 128), np.float32)
    for h in range(2):
        bones[h * 64:(h + 1) * 64, h * 64:(h + 1) * 64] = 1.0
    reset = np.ones((128, 256), np.float32)
    reset[:, ::64] = 0.0
    return {"rmask": rmask.astype(ml_dtypes.bfloat16), "identb": identb.astype(ml_dtypes.bfloat16), "bones": bones, "resetm": reset}


def core_inputs(inp, i, shared):
    xin = np.concatenate([inp["x_sample"][i], inp["x_prompt"][2 * i], inp["x_prompt"][2 * i + 1]], axis=0)
    condT = np.stack([fm(inp["c"][i]), fm(inp["c_ctx"])], axis=-1)
    m = {"xin": np.ascontiguousarray(xin, np.float32), "condT": np.ascontiguousarray(condT, np.float32),
         "cache_k": np.ascontiguousarray(inp["cache_k"][i, 0].reshape(512, 512), np.float32),
         "cache_v": np.ascontiguousarray(inp["cache_v"][i, 0].reshape(512, 512), np.float32),
         "st_in": np.ascontiguousarray(inp["state_rwkv"][i, 0].reshape(2, 2048, 64), np.float32)}
    m.update(shared)
    return m


def shared_inputs(inp):
    return {
        "vecs": build_vecs(inp),
        "ident": np.eye(128, dtype=np.float32),
        "ada_w": np.asarray(inp["ada_w"], np.float32),
        "ffn_up": np.asarray(inp["ffn_up"], np.float32),
        "ffn_down": np.asarray(inp["ffn_down"], np.float32),
        "pool_w": np.asarray(inp["pool_w"], np.float32),
        "w_qkv": np.asarray(inp["attn_w_qkv"][0], np.float32),
        "w_ao": np.asarray(inp["attn_w_o"][0], np.float32),
        **attn_consts(),
        **rwkv_consts(),
        "rw_r": np.asarray(inp["rwkv_w_r"][0], np.float32), "rw_k": np.asarray(inp["rwkv_w_k"][0], np.float32),
        "rw_v": np.asarray(inp["rwkv_w_v"][0], np.float32), "rw_o": np.asarray(inp["rwkv_w_o"][0], np.float32),
        "rw_w1": np.asarray(inp["rwkv_w1"][0], np.float32), "rw_w2": np.asarray(inp["rwkv_w2"][0], np.float32),
        "rw_a1": np.asarray(inp["rwkv_a1"][0], np.float32), "rw_a2": np.asarray(inp["rwkv_a2"][0], np.float32),
        "rw_g1": np.asarray(inp["rwkv_g1"][0], np.float32), "rw_g2": np.asarray(inp["rwkv_g2"][0], np.float32),
    }


def kernel(**inputs):
    inp = {k: np.asarray(v) for k, v in inputs.items()}
    nc = build()
    shared = shared_inputs(inp)
    in_maps = [core_inputs(inp, i, shared) for i in range(8)]
    res = run_bass_kernel_spmd(nc, in_maps, core_ids=list(range(8)))
    ys = [r["y"] for r in res.results]
    y_sample = np.stack([ys[i][0:2048] for i in range(8)], axis=0)
    y_prompt = np.stack([ys[i // 2][2048 + 256 * (i % 2):2048 + 256 * (i % 2) + 256] for i in range(16)], axis=0)
    new_state = np.stack([res.results[i // 2]["st_out"][i % 2].reshape(2, 32, 64, 64) for i in range(16)], axis=0)[:, None]
    new_ck = np.stack([res.results[i // 2]["ck_out"][i % 2].reshape(256, 4, 128) for i in range(16)], axis=0)[:, None]
    new_cv = np.stack([res.results[i // 2]["cv_out"][i % 2].reshape(256, 4, 128) for i in range(16)], axis=0)[:, None]
    f32 = lambda a: np.ascontiguousarray(a, dtype=np.float32)
    return f32(y_prompt), f32(y_sample), f32(new_state), f32(new_ck), f32(new_cv)
```

```python
from contextlib import ExitStack
import numpy as np
import concourse.bass as bass
import concourse.mybir as mybir
from concourse.bass_utils import run_bass_kernel_spmd

F32 = mybir.dt.float32
BF16 = mybir.dt.bfloat16
AF = mybir.ActivationFunctionType
ALU = mybir.AluOpType

PE, ACT, DVE, POOL, SP = "tensor", "scalar", "vector", "gpsimd", "sync"
ENGS = [PE, ACT, DVE, POOL, SP]
SEM_ROLL = 20000

D = 2048
KC = 16
NTOK = 2560
DFF = 5632
FC = 44
DEPTH = 4
MIXER = (0, 1, 2, 0)
SLOT = (0, 0, 0, 1)
SEGS = [(0, 512, 0), (512, 512, 0), (1024, 512, 0), (1536, 512, 0), (2048, 512, 1)]
SEQS = [(0, 2048, 0), (2048, 256, 1), (2304, 256, 1)]
NORM_EPS = 1e-6


class Buf:
    __slots__ = ("name", "w", "r", "dsem", "const", "excl")

    def __init__(self, name, const=False, excl=False):
        self.name = name
        self.w = None
        self.r = {}
        self.dsem = None
        self.const = const
        self.excl = excl


class Prog:
    def __init__(self, nc, stack):
        self.nc = nc
        self.stack = stack
        self.streams = {e: [] for e in ENGS}
        self.cnt = {e: 0 for e in ENGS}
        self.sems = {}
        self.dma_tot = {}
        self.waited = {e: {} for e in ENGS}
        self.ndma = 0

    def _sem(self, key):
        if key not in self.sems:
            self.sems[key] = self.stack.enter_context(self.nc.semaphore("s_" + "_".join(str(k) for k in key)))
        return self.sems[key]

    def _waits(self, eng, deps):
        best = {}
        for (k, v) in deps:
            if k[0] == "dma":
                v = self.dma_tot[k]
            elif k[1] == PE and eng == PE:
                continue
            if v > best.get(k, 0):
                best[k] = v
        out = []
        wd = self.waited[eng]
        for k, v in best.items():
            if wd.get(k, 0) >= v:
                continue
            wd[k] = v
            out.append((k, v))
        return out

    def _deps(self, reads, writes):
        deps = []
        for b in reads:
            if b.w is not None:
                deps.append(b.w)
            if b.excl:
                deps.extend(b.r.items())
        for b in writes:
            if b.w is not None:
                deps.append(b.w)
            deps.extend(b.r.items())
        return deps

    def _record(self, ev, reads, writes):
        for b in reads:
            if b.excl:
                b.w = ev
                b.r = {}
            elif not b.const:
                if ev[1] > b.r.get(ev[0], 0):
                    b.r[ev[0]] = ev[1]
        for b in writes:
            b.w = ev
            b.r = {}

    def op(self, eng, fn, reads=(), writes=()):
        waits = self._waits(eng, self._deps(reads, writes))
        self.cnt[eng] += 1
        c = self.cnt[eng]
        key = ("eng", eng, (c - 1) // SEM_ROLL)
        ev = (key, (c - 1) % SEM_ROLL + 1)
        self._sem(key)
        self.streams[eng].append((waits, fn, ev, 1))
        self._record(ev, reads, writes)
        return ev

    def dma(self, q, fn, reads=(), writes=(), chan=None):
        waits = self._waits(q, self._deps(reads, writes))
        if chan is None:
            chan = writes[0]
        if chan.dsem is None:
            chan.dsem = ("dma", self.ndma)
            self.ndma += 1
            self.dma_tot[chan.dsem] = 0
            self._sem(chan.dsem)
        key = chan.dsem
        self.dma_tot[key] += 16
        ev = (key, self.dma_tot[key])
        self.streams[q].append((waits, fn, ev, 16))
        self._record(ev, reads, writes)
        return ev

    def barrier(self):
        allev = []
        for e in ENGS:
            c = self.cnt[e]
            if c:
                allev.append((("eng", e, (c - 1) // SEM_ROLL), (c - 1) % SEM_ROLL + 1))
        for k, v in self.dma_tot.items():
            if v:
                allev.append((k, v))
        for e in ENGS:
            waits = self._waits(e, [x for x in allev])
            if waits:
                self.streams[e].append((waits, None, None, 0))

    def emit(self):
        nc = self.nc
        with nc.Block() as block:
            for e in ENGS:
                stream = self.streams[e]

                def body(eng, stream=stream):
                    for (waits, fn, ev, inc) in stream:
                        for (k, v) in waits:
                            eng.wait_ge(self.sems[k], v)
                        if fn is not None:
                            fn(eng).then_inc(self.sems[ev[0]], inc)
                getattr(block, e)(body)


class T:
    def __init__(self, h, name, const=False):
        self.h = h
        self.b = Buf(name, const)

    def __getitem__(self, idx):
        return self.h[idx]


def fm(v):
    v = np.asarray(v, np.float32).reshape(-1, 128)
    return np.ascontiguousarray(v.T)


def vec_layout():
    off = {}
    pos = 0

    def add(name, n):
        nonlocal pos
        off[name] = (pos, n)
        pos += n
    for l in range(DEPTH):
        add(f"adab{l}", 96)
        add(f"nm{l}", 16)
        add(f"nf{l}", 16)
        for j in range(3):
            add(f"cw{l}_{j}", 88)
        add(f"cb{l}", 88)
    for s in range(2):
        add(f"pscale{s}", 16)
    add("pool_icnt", 64)
    add("qn", 1)
    add("kn", 1)
    add("sink", 16)
    for i in range(6):
        add(f"mu{i}", 16)
    for e in range(2):
        add(f"w0_{e}", 16)
        add(f"a0_{e}", 16)
    for nm_ in ("k_k", "k_a", "r_k", "ln_w", "ln_b"):
        add(nm_, 16)
    off["_total"] = (pos, 0)
    return off


VOFF = vec_layout()


def build_vecs(inp):
    tot = VOFF["_total"][0]
    vecs = np.zeros((128, tot), np.float32)

    def put(name, arr):
        o, n = VOFF[name]
        assert arr.shape == (128, n), (name, arr.shape, n)
        vecs[:, o:o + n] = arr
    for l in range(DEPTH):
        put(f"adab{l}", fm(inp["ada_b"][l]))
        put(f"nm{l}", fm(inp["norm_mix"][l]))
        put(f"nf{l}", fm(inp["norm_ffn"][l]))
        for j in range(3):
            put(f"cw{l}_{j}", fm(inp["ffn_conv_w"][l][j]))
        put(f"cb{l}", fm(inp["ffn_conv_b"][l]))
    for s in range(2):
        put(f"pscale{s}", fm(inp["pool_scale"][s]))
    ic = np.zeros((4, 16), np.float32)
    for gi, win in enumerate((2, 4, 8, 16)):
        left = win // 2
        right = win - 1 - left
        for t in range(left):
            ic[gi, t] = 1.0 / (t + right + 1)
        for j in range(right):
            ic[gi, 8 + j] = 1.0 / (j + 1 + left)
    put("pool_icnt", np.broadcast_to(ic.reshape(1, 64), (128, 64)).copy())
    put("qn", np.asarray(inp["attn_q_norm"][0], np.float32).reshape(128, 1))
    put("kn", np.asarray(inp["attn_k_norm"][0], np.float32).reshape(128, 1))
    for i in range(6):
        put(f"mu{i}", fm(inp["rwkv_mu"][0][i]))
    for e in range(2):
        put(f"w0_{e}", fm(inp["rwkv_w0"][0][e]))
        put(f"a0_{e}", fm(inp["rwkv_a0"][0][e]))
    put("k_k", fm(inp["rwkv_k_k"][0]))
    put("k_a", fm(inp["rwkv_k_a"][0]))
    put("r_k", fm(inp["rwkv_r_k"][0].reshape(-1)))
    put("ln_w", fm(inp["rwkv_ln_w"][0]))
    put("ln_b", fm(inp["rwkv_ln_b"][0]))
    put("sink", np.broadcast_to(np.asarray(inp["attn_sink"][0], np.float32).reshape(1, 16), (128, 16)).copy())
    return vecs


def build(layers=(0, 1, 2, 3), mixers=True, ffn=True):
    nc = bass.Bass("TRN2", target_bir_lowering=False)
    dr = lambda name, shape, dt=F32, kind="ExternalInput": nc.dram_tensor(name, list(shape), dt, kind=kind).ap()
    xin = dr("xin", [NTOK, D])
    condT = dr("condT", [128, KC, 2])
    vecs_d = dr("vecs", [128, VOFF["_total"][0]])
    ident_d = dr("ident", [128, 128])
    ada_w = dr("ada_w", [DEPTH, D, 6 * D])
    ffn_up = dr("ffn_up", [DEPTH, D, 2 * DFF])
    ffn_down = dr("ffn_down", [DEPTH, DFF, D])
    pool_w = dr("pool_w", [2, 4, 512, 512])
    w_qkv = dr("w_qkv", [D, 3072])
    w_ao = dr("w_ao", [D, D])
    cache_k = dr("cache_k", [512, 512])
    cache_v = dr("cache_v", [512, 512])
    amask_d = dr("amask", [128, 6, 512], BF16)
    rotT_d = dr("rotT", [128, 128])
    ropeC_d = dr("ropeC", [128, 2048])
    ropeS_d = dr("ropeS", [128, 2048])
    ck_out = dr("ck_out", [2, 256, 512], kind="ExternalOutput")
    rw_r = dr("rw_r", [D, D])
    rw_k = dr("rw_k", [D, D])
    rw_v = dr("rw_v", [D, D])
    rw_o = dr("rw_o", [D, D])
    rw_w1 = dr("rw_w1", [2, D, 96])
    rw_w2 = dr("rw_w2", [2, 96, D])
    rw_a1 = dr("rw_a1", [2, D, 96])
    rw_a2 = dr("rw_a2", [2, 96, D])
    rw_g1 = dr("rw_g1", [D, 256])
    rw_g2 = dr("rw_g2", [256, D])
    st_in = dr("st_in", [2, D, 64])
    rmask_d = dr("rmask", [4, 128, 4, 128], BF16)
    identb_d = dr("identb", [128, 4, 128], BF16)
    bones_d = dr("bones", [128, 128])
    reset_d = dr("resetm", [128, 256])
    st_out = dr("st_out", [2, 2, D, 64], kind="ExternalOutput")
    import os as _os2
    _dbg = _os2.environ.get("RW_DEBUG") == "1"
    osc = dr("osc", [KC, 128, NTOK], kind="ExternalOutput" if _dbg else "Internal").rearrange("c p t -> p c t")
    bsc = dr("bsc", [KC, 128, NTOK], kind="ExternalOutput" if _dbg else "Internal").rearrange("c p t -> p c t")
    cv_out = dr("cv_out", [2, 256, 512], kind="ExternalOutput")
    y = dr("y", [NTOK, D], kind="ExternalOutput")
    scr2 = [dr(f"scr{i}", [KC, 128, NTOK], kind="Internal") for i in range(2)]
    scr_v2 = [s_.rearrange("c p t -> p c t") for s_ in scr2]

    with ExitStack() as st:
        P = Prog(nc, st)

        uniq = [0]

        def sb(name, shape, dt=F32, const=False, stack=st):
            uniq[0] += 1
            return T(stack.enter_context(nc.sbuf_tensor(f"t{uniq[0]}_{name}", list(shape), dt)), name, const)

        ident = sb("ident", [128, 128], F32)
        ones = sb("ones", [128, 128], F32)
        vecs = sb("vecs", [128, VOFF["_total"][0]], F32)
        cond = sb("cond", [128, KC, 2], F32)
        scond = sb("scond", [128, KC, 2], BF16)
        mod = sb("mod", [128, 96, 2], F32)
        gs1 = sb("gs1", [128, KC, 2], F32)
        gs2 = sb("gs2", [128, KC, 2], F32)
        gpl = sb("gpl", [128, KC, 2], F32)
        psum = [T(st.enter_context(nc.psum_tensor(f"ps{i}", [128, 512], F32)), f"ps{i}") for i in range(8)]
        for p_ in psum:
            p_.b.excl = True
        scr_b2 = [[Buf(f"scr{i}_{s}") for s in range(5)] for i in range(2)]
        scur = 0
        scr_v, scr_b = scr_v2[0], scr_b2[0]
        y_b = Buf("y")

        def V(name, j=None, n=None):
            o, nn = VOFF[name]
            if j is None:
                return vecs[:, o:o + nn]
            return vecs[:, o + j:o + j + (n or 1)]

        def mm(out, lhsT, rhs, start, stop, R, W):
            P.op(PE, lambda e: e.matmul(out, lhsT=lhsT, rhs=rhs, start=start, stop=stop), reads=R, writes=W)

        def tr(out, in_, R, W):
            P.op(PE, lambda e: e.transpose(out=out, in_=in_, identity=ident[:]), reads=R + [ident.b], writes=W)

        def act(out, in_, func, R, W, scale=1.0, bias=0.0):
            P.op(ACT, lambda e: e.activation(out=out, in_=in_, func=func, bias=bias, scale=scale), reads=R, writes=W)

        def tt(eng, out, in0, in1, op, R, W):
            P.op(eng, lambda e: e.tensor_tensor(out=out, in0=in0, in1=in1, op=op), reads=R, writes=W)

        def ts(eng, out, in0, s1, op0, R, W, s2=None, op1=None):
            if op1 is None:
                P.op(eng, lambda e: e.tensor_scalar(out=out, in0=in0, scalar1=s1, scalar2=None, op0=op0), reads=R, writes=W)
            else:
                P.op(eng, lambda e: e.tensor_scalar(out=out, in0=in0, scalar1=s1, scalar2=s2, op0=op0, op1=op1), reads=R, writes=W)

        def stt(out, in0, scalar, in1, op0, op1, R, W):
            P.op(DVE, lambda e: e.scalar_tensor_tensor(out=out, in0=in0, scalar=scalar, in1=in1, op0=op0, op1=op1), reads=R, writes=W)

        def cp(eng, out, in_, R, W):
            if eng == ACT:
                act(out, in_, AF.Copy, R, W)
            else:
                P.op(eng, lambda e: e.tensor_copy(out=out, in_=in_), reads=R, writes=W)

        def memset(eng, ap, val, W):
            P.op(eng, lambda e: e.memset(ap, val), writes=W)

        def dma(q, out, in_, R, W, chan=None):
            P.dma(q, lambda e: e.dma_start(out=out, in_=in_), reads=R, writes=W, chan=chan)

        dma(SP, ident[:], ident_d, [], [ident.b])
        dma(SP, vecs[:], vecs_d, [], [vecs.b])
        dma(SP, cond[:], condT, [], [cond.b])
        memset(DVE, ones[:], 1.0, [ones.b])
        act(scond[:], cond[:], AF.Silu, [cond.b], [scond.b])

        with ExitStack() as ph:
            xt = [sb(f"xt{i}", [128, D], F32, stack=ph) for i in range(2)]
            xst = [sb(f"xst{i}", [128, KC, 128], F32, stack=ph) for i in range(2)]
            for t_ in range(NTOK // 128):
                a, s_ = xt[t_ % 2], xst[t_ % 2]
                dma(SP, a[:], xin[t_ * 128:(t_ + 1) * 128, :], [], [a.b])
                for g in range(4):
                    pb = psum[(t_ * 4 + g) % 8]
                    for j in range(4):
                        kc = g * 4 + j
                        tr(pb[:, j * 128:(j + 1) * 128], a[:, kc * 128:(kc + 1) * 128], [a.b], [pb.b])
                    cp(ACT if g % 2 == 0 else DVE, s_[:, g * 4:(g + 1) * 4, :],
                       pb[:].rearrange("p (a b) -> p a b", a=4), [pb.b], [s_.b])
                dma(SP, scr_v[:, :, t_ * 128:(t_ + 1) * 128], s_[:], [s_.b], [scr_b[t_ // 4]], chan=s_.b)
        P.barrier()

        def rstd_cols(xv, ncols, r_ap, r_b, x_b, sqt, pb):
            for kc in range(KC):
                q = sqt[kc % 2]
                act(q[:, 0:ncols], xv(kc), AF.Square, [x_b], [q.b])
                mm(pb[:, 0:ncols], ones[:], q[:, 0:ncols], kc == 0, kc == KC - 1, [ones.b, q.b], [pb.b])
            act(r_ap, pb[:, 0:ncols], AF.Sqrt, [pb.b], [r_b], scale=1.0 / D, bias=epsb[:, 0:1])
            P.op(DVE, lambda e: e.reciprocal(out=r_ap, in_=r_ap), reads=[r_b], writes=[r_b])

        epsb = sb("epsb", [128, 1], F32)
        memset(DVE, epsb[:], NORM_EPS, [epsb.b])
        gneps = sb("gneps", [128, 1], F32)
        memset(DVE, gneps[:], 64e-5, [gneps.b])

        for l in layers:
            kind, slot = MIXER[l], SLOT[l]
            with ExitStack() as ph:
                wts = [sb(f"aw{i}", [128, KC, 512], BF16, stack=ph) for i in range(3)]
                aw_v = ada_w[l].rearrange("(c p) n -> p c n", p=128)
                NST = 24

                def aload(i):
                    w = wts[i % 3]
                    dma(POOL, w[:], aw_v[:, :, i * 512:(i + 1) * 512], [], [w.b])
                aload(0)
                for i in range(NST):
                    if i + 1 < NST:
                        aload(i + 1)
                    w = wts[i % 3]
                    pb = psum[i % 4]
                    for o4 in range(4):
                        for kc in range(KC):
                            mm(pb[:, o4 * 2:o4 * 2 + 2], w[:, kc, o4 * 128:(o4 + 1) * 128], scond[:, kc, :],
                               kc == 0, kc == KC - 1, [w.b, scond.b], [pb.b])
                    for o4 in range(4):
                        oc = i * 4 + o4
                        act(mod[:, oc, :], pb[:, o4 * 2:o4 * 2 + 2], AF.Identity, [pb.b, vecs.b], [mod.b],
                            bias=V(f"adab{l}", oc))
                for cc in range(2):
                    stt(gs1[:, :, cc], mod[:, 16:32, cc], 1.0, V(f"nm{l}"), ALU.add, ALU.mult, [mod.b, vecs.b], [gs1.b])
                    stt(gs2[:, :, cc], mod[:, 64:80, cc], 1.0, V(f"nf{l}"), ALU.add, ALU.mult, [mod.b, vecs.b], [gs2.b])
                    if kind == 0:
                        tt(DVE, gpl[:, :, cc], mod[:, 32:48, cc], V(f"pscale{slot}"), ALU.mult, [mod.b, vecs.b], [gpl.b])
            P.barrier()

            if mixers and kind == 0:
                with ExitStack() as ph:
                    r_all = sb("r_all", [128, NTOK], F32, stack=ph)
                    xs = sb("pxs", [128, KC, 512], F32, stack=ph)
                    sqt = [sb(f"psq{i}", [128, 512], F32, stack=ph) for i in range(2)]
                    for si, (s0, sl, cc) in enumerate(SEGS):
                        dma(SP, xs[:], scr_v[:, :, s0:s0 + sl], [scr_b[si]], [xs.b])
                        rstd_cols(lambda kc: xs[:, kc, :], 512, r_all[:, s0:s0 + sl], r_all.b, xs.b, sqt, psum[si % 2])
                    xg = sb("xg", [128, 4, NTOK], F32, stack=ph)
                    pooled = sb("pooled", [128, 4, NTOK], BF16, stack=ph)
                    hp = [sb(f"hp{i}", [128, 2064], F32, stack=ph) for i in range(2)]
                    tmp = [sb(f"ptmp{i}", [128, 2048], F32, stack=ph) for i in range(2)]
                    sA = sb("sA", [128, 2064], F32, stack=ph)
                    sB = sb("sB", [128, 2064], F32, stack=ph)
                    pw = [sb(f"pw{i}", [128, 4, 512], BF16, stack=ph) for i in range(2)]
                    it = 0
                    for gi, win in enumerate((2, 4, 8, 16)):
                        left = win // 2
                        right = win - 1 - left
                        w = pw[gi % 2]
                        dma(POOL, w[:], pool_w[slot, gi].rearrange("(c p) n -> p c n", p=128), [], [w.b])
                        dma(SP, xg[:], scr_v[:, 4 * gi:4 * gi + 4, :], scr_b, [xg.b])
                        for k4 in range(4):
                            kc = 4 * gi + k4
                            for (q0, Tn, cc) in SEQS:
                                h = hp[it % 2]
                                tm = tmp[it % 2]
                                it += 1
                                Wd = Tn + 16
                                memset(DVE, h[:, 0:8], 0.0, [h.b])
                                memset(DVE, h[:, Tn + 8:Tn + 16], 0.0, [h.b])
                                stt(tm[:, 0:Tn], xg[:, k4, q0:q0 + Tn], gs1[:, kc, cc:cc + 1], r_all[:, q0:q0 + Tn],
                                    ALU.mult, ALU.mult, [xg.b, gs1.b, r_all.b], [tm.b])
                                act(h[:, 8:8 + Tn], tm[:, 0:Tn], AF.Identity, [tm.b, mod.b], [h.b], bias=mod[:, kc, cc:cc + 1])
                                tt(DVE, sA[:, 1:Wd], h[:, 1:Wd], h[:, 0:Wd - 1], ALU.add, [h.b], [sA.b])
                                cur, lo, hi = sA, 1, Wd
                                oth = sB
                                for sh_ in (1, 2, 4):
                                    if win <= 2 * sh_:
                                        break
                                    tt(DVE, oth[:, lo + sh_:hi - sh_], cur[:, lo:hi - 2 * sh_], cur[:, lo + 2 * sh_:hi], ALU.add,
                                       [cur.b], [oth.b])
                                    cur, oth = oth, cur
                                    lo, hi = lo + sh_, hi - sh_
                                po = pooled[:, k4, q0:q0 + Tn]
                                stt(po, cur[:, 8:8 + Tn], 1.0 / win, h[:, 8:8 + Tn], ALU.mult, ALU.subtract, [cur.b, h.b], [pooled.b])
                                o_ic = VOFF["pool_icnt"][0] + gi * 16
                                tt(DVE, tm[:, 0:left], cur[:, 8:8 + left], vecs[:, o_ic:o_ic + left], ALU.mult, [cur.b, vecs.b], [tm.b])
                                tt(DVE, pooled[:, k4, q0:q0 + left], tm[:, 0:left], h[:, 8:8 + left], ALU.subtract, [tm.b, h.b], [pooled.b])
                                if right > 0:
                                    for j in range(right):
                                        c_ = Tn - 1 - j
                                        stt(pooled[:, k4, q0 + c_:q0 + c_ + 1], cur[:, 8 + c_:8 + c_ + 1], vecs[:, o_ic + 8 + j:o_ic + 9 + j],
                                            h[:, 8 + c_:8 + c_ + 1], ALU.mult, ALU.subtract, [cur.b, h.b, vecs.b], [pooled.b])
                        n_ = 0
                        for o4 in range(4):
                            kc = 4 * gi + o4
                            for si, (s0, sl, cc) in enumerate(SEGS):
                                pb = psum[2 + n_ % 6]
                                n_ += 1
                                for ic in range(4):
                                    mm(pb[:, 0:sl], w[:, ic, o4 * 128:(o4 + 1) * 128], pooled[:, ic, s0:s0 + sl], ic == 0, ic == 3,
                                       [w.b, pooled.b], [pb.b])
                                stt(xg[:, o4, s0:s0 + sl], pb[:, 0:sl], gpl[:, kc, cc:cc + 1], xg[:, o4, s0:s0 + sl], ALU.mult, ALU.add,
                                    [pb.b, gpl.b, xg.b], [xg.b])
                        dma(SP, scr_v[:, 4 * gi:4 * gi + 4, :], xg[:], [xg.b], scr_b, chan=xg.b)
                P.barrier()


            if mixers and kind == 1:
                with ExitStack() as ph:
                    CD = 0.606531
                    NB = 256

                    class TV:
                        def __init__(self, ap, b_):
                            self.ap, self.b = ap, b_

                        def __getitem__(self, idx):
                            return self.ap[idx]
                    rmask = [sb(f"w_mask{i}", [128, 4, 128], BF16, stack=ph) for i in range(4)]
                    identb = sb("w_identb", [128, 4, 128], BF16, stack=ph)
                    bones = sb("w_bones", [128, 128], F32, stack=ph)
                    resetm = sb("w_reset", [128, NB], F32, stack=ph)
                    omka = sb("w_omka", [128, KC], F32, stack=ph)
                    for i in range(4):
                        dma(SP, rmask[i][:], rmask_d[i], [], [rmask[i].b])
                    dma(SP, identb[:], identb_d, [], [identb.b])
                    dma(SP, bones[:], bones_d, [], [bones.b])
                    dma(SP, resetm[:], reset_d, [], [resetm.b])
                    ts(DVE, omka[:], V("k_a"), -1.0, ALU.mult, [vecs.b], [omka.b], s2=1.0, op1=ALU.add)
                    Mf = sb("w_Mf", [128, KC, 128], F32, stack=ph)
                    Mb = sb("w_Mb", [128, KC, 128], BF16, stack=ph)
                    xsraw = sb("w_xs", [128, KC * (NB + 2)], F32, stack=ph)
                    xs = TV(xsraw.h[:, :].rearrange("p (k n) -> p k n", k=KC), xsraw.b)
                    yb = TV(xsraw.h.bitcast(BF16)[:, 0:KC * NB].rearrange("p (k n) -> p k n", k=KC), xsraw.b)
                    rr = sb("w_rr", [128, NB + 2], F32, stack=ph)
                    sqt = [sb(f"w_sq{i}", [128, NB + 2], F32, stack=ph) for i in range(2)]
                    hk = [sb(f"w_hk{i}", [128, NB + 2], F32, stack=ph) for i in range(2)]
                    t1k = [sb(f"w_t1k{i}", [128, NB + 2], F32, stack=ph) for i in range(2)]
                    xxk = [sb(f"w_xxk{i}", [128, NB], F32, stack=ph) for i in range(2)]
                    Xr = sb("w_Xr", [128, KC, NB], BF16, stack=ph)
                    Xk = sb("w_Xk", [128, KC, NB], BF16, stack=ph)
                    Xv = sb("w_Xv", [128, KC, NB], BF16, stack=ph)
                    Xsm = [[sb(f"w_Xs{j}{i}", [128, NB], BF16, stack=ph) for i in range(2)] for j in range(3)]
                    Lw = sb("w_Lw", [96, NB], BF16, stack=ph)
                    La = sb("w_La", [96, NB], BF16, stack=ph)
                    Lg = sb("w_Lg", [128, 2, NB], BF16, stack=ph)
                    w1t = sb("w_w1t", [128, KC, 96], BF16, stack=ph)
                    a1t = sb("w_a1t", [128, KC, 96], BF16, stack=ph)
                    w2t = sb("w_w2t", [96, 512], BF16, stack=ph)
                    a2t = sb("w_a2t", [96, 512], BF16, stack=ph)
                    g2t = sb("w_g2t", [128, 2, 512], BF16, stack=ph)
                    wbig = [sb(f"w_wbig{i}", [128, KC, 128], BF16, stack=ph) for i in range(6)]
                    xres = [sb(f"w_xres{i}", [128, NB], F32, stack=ph) for i in range(2)]
                    pts = [{n_: sb(f"w_p{j_}_" + n_, [128, NB], F32, stack=ph) for n_ in
                            ("rf", "kf", "vf", "sg", "af", "kk", "kd", "cum", "Ep", "Em", "Eex", "u1", "u2", "u3")} for j_ in range(2)]
                    pt = pts[0]
                    gC = sb("w_gC", [128, 4, 4], F32, stack=ph)
                    BDn = ("aT", "bT", "kT", "rT", "vT")
                    BD = {n_: sb("w_bd_" + n_, [128, 4, 4, 128], BF16, stack=ph) for n_ in BDn}
                    for n_ in BDn:
                        memset(DVE, BD[n_][:], 0.0, [BD[n_].b])
                    o_fm = sb("w_ofm", [128, 4, NB], F32, stack=ph)
                    bon = sb("w_bon", [128, 4, NB], F32, stack=ph)
                    NSET = 2
                    cs_b16 = ("A_tm", "B_tm", "K_tm", "V_tm", "P0", "P1", "Q0", "Q1", "N_ak", "N_rb", "N_rk", "Xb", "Xc")
                    CS = [{n_: sb(f"w_c{i}_{n_}", [128, 4, 128], BF16, stack=ph) for n_ in cs_b16} for i in range(NSET)]
                    alias = {"X1": "P0", "Ul": "P1", "Ah": "Q0", "RhT": "Q1", "GpT": "N_ak"}
                    stt_in = sb("w_stin", [128, 64], F32, stack=ph)
                    sbd = sb("w_sbd", [128, 128], F32, stack=ph)
                    memset(DVE, sbd[:], 0.0, [sbd.b])
                    so = [sb(f"w_so{i}", [128, 128], F32, stack=ph) for i in range(2)]
                    st_b = Buf("st_out")
                    osc_b, bsc_b = Buf("osc"), Buf("bsc")
                    pctr = [0]

                    def pnext():
                        pb = psum[pctr[0] % 8]
                        pctr[0] += 1
                        return pb

                    def mm4(pb, L, R, start=True, stop=True, Lfix=None, Rfix=None):
                        for qi in range(4):
                            l_ap = L[0][:, qi, :] if Lfix is None else Lfix[qi]
                            r_ap = R[0][:, qi, :] if Rfix is None else Rfix[qi]
                            mm(pb[:, qi * 128:(qi + 1) * 128], l_ap, r_ap, start, stop, [L[1], R[1]], [pb.b])

                    def mm4multi(pb, terms):
                        for qi in range(4):
                            for ti, (L, R, Lfix, Rfix) in enumerate(terms):
                                l_ap = L[0][:, qi, :] if Lfix is None else Lfix[qi]
                                r_ap = R[0][:, qi, :] if Rfix is None else Rfix[qi]
                                mm(pb[:, qi * 128:(qi + 1) * 128], l_ap, r_ap, ti == 0, ti == len(terms) - 1, [L[1], R[1]], [pb.b])

                    def interleave(gens):
                        gens = list(gens)
                        while gens:
                            for g_ in list(gens):
                                try:
                                    next(g_)
                                except StopIteration:
                                    gens.remove(g_)

                    def pv(pb):
                        return pb[:, 0:512].rearrange("p (q n) -> p q n", q=4)

                    def half(hh):
                        return slice(hh * 64, (hh + 1) * 64)

                    blocks = [(0, 2048, t0) for t0 in range(0, 2048, NB)] + [(2048, 256, 0), (2304, 256, 0)]
                    wr_v = rw_r.rearrange("(c p) n -> p c n", p=128)
                    wk_v = rw_k.rearrange("(c p) n -> p c n", p=128)
                    wv_v = rw_v.rearrange("(c p) n -> p c n", p=128)
                    wo_v = rw_o.rearrange("(c p) n -> p c n", p=128)
                    g1_v = rw_g1.rearrange("(c p) n -> p c n", p=128)
                    wbc = [0]

                    def wbnext():
                        w = wbig[wbc[0] % 6]
                        wbc[0] += 1
                        return w

                    for e in range(2):
                        dma(POOL, w1t[:], rw_w1[e].rearrange("(c p) n -> p c n", p=128), [], [w1t.b])
                        dma(POOL, a1t[:], rw_a1[e].rearrange("(c p) n -> p c n", p=128), [], [a1t.b])
                        mBef, mBefT, mInc = (rmask[0], rmask[1], rmask[2]) if e == 0 else (rmask[1], rmask[0], rmask[3])
                        order = blocks if e == 0 else (blocks[:8][::-1] + blocks[8:])
                        for (q0, Tn, t0) in order:
                            cc = 0 if q0 == 0 else 1
                            g0 = q0 + t0
                            first_of_seq = (t0 == 0) if e == 0 else (t0 + NB == Tn)
                            last_of_seq = (t0 + NB == Tn) if e == 0 else (t0 == 0)
                            if first_of_seq:
                                if cc == 0:
                                    for p_ in range(KC):
                                        dma(SP, stt_in[:], st_in[e, p_ * 128:(p_ + 1) * 128, :], [], [stt_in.b])
                                        for hh in range(2):
                                            cp(DVE, sbd[half(hh), hh * 64:(hh + 1) * 64], stt_in[half(hh), :], [stt_in.b], [sbd.b])
                                        pb = pnext()
                                        tr(pb[:, 0:128], sbd[:], [sbd.b], [pb.b])
                                        cp(ACT, Mf[:, p_, :], pb[:, 0:128], [pb.b], [Mf.b])
                                    cp(DVE, Mb[:], Mf[:], [Mf.b], [Mb.b])
                                else:
                                    memset(DVE, Mf[:], 0.0, [Mf.b])
                                    memset(DVE, Mb[:], 0.0, [Mb.b])
                            lo_t, hi_t = max(t0 - 1, 0), min(t0 + NB + 1, Tn)
                            lo_c = 1 - (t0 - lo_t)
                            hi_c = lo_c + (hi_t - lo_t)
                            dma(SP, xs[:, :, lo_c:hi_c], scr_v[:, :, q0 + lo_t:q0 + hi_t], scr_b, [xs.b])
                            rstd_cols(lambda kc: xs[:, kc, lo_c:hi_c], hi_c - lo_c, rr[:, lo_c:hi_c], rr.b, xs.b, sqt, pnext())
                            if e == 1:
                                g1t = [wbnext(), wbnext()]
                                for j2 in range(2):
                                    dma(POOL, g1t[j2][:], g1_v[:, :, j2 * 128:(j2 + 1) * 128], [], [g1t[j2].b])
                            p_lw, p_la, p_lg = psum[4], psum[5], (psum[6], psum[7])
                            for kc in range(KC):
                                h_, t1_, xx_ = hk[kc % 2], t1k[kc % 2], xxk[kc % 2]
                                stt(t1_[:, lo_c:hi_c], xs[:, kc, lo_c:hi_c], gs1[:, kc, cc:cc + 1], rr[:, lo_c:hi_c], ALU.mult, ALU.mult,
                                    [xs.b, gs1.b, rr.b], [t1_.b])
                                act(h_[:, lo_c:hi_c], t1_[:, lo_c:hi_c], AF.Identity, [t1_.b, mod.b], [h_.b], bias=mod[:, kc, cc:cc + 1])
                                if lo_c == 1:
                                    memset(DVE, h_[:, 0:1], 0.0, [h_.b])
                                if hi_c == NB + 1:
                                    memset(DVE, h_[:, NB + 1:NB + 2], 0.0, [h_.b])
                                tt(DVE, t1_[:, 0:NB], h_[:, 0:NB], h_[:, 2:NB + 2], ALU.add, [h_.b], [t1_.b])
                                stt(xx_[:], t1_[:, 0:NB], 0.5, h_[:, 1:NB + 1], ALU.mult, ALU.subtract, [t1_.b, h_.b], [xx_.b])
                                for (mi, Xt) in ((0, Xr), (2, Xk), (3, Xv)):
                                    stt(Xt[:, kc, :], xx_[:], V(f"mu{mi}", kc), h_[:, 1:NB + 1], ALU.mult, ALU.add, [xx_.b, vecs.b, h_.b], [Xt.b])
                                xw_, xa_, xg_ = Xsm[0][kc % 2], Xsm[1][kc % 2], Xsm[2][kc % 2]
                                stt(xw_[:], xx_[:], V("mu1", kc), h_[:, 1:NB + 1], ALU.mult, ALU.add, [xx_.b, vecs.b, h_.b], [xw_.b])
                                mm(p_lw[0:96, 0:NB], w1t[:, kc, :], xw_[:], kc == 0, kc == KC - 1, [w1t.b, xw_.b], [p_lw.b])
                                stt(xa_[:], xx_[:], V("mu4", kc), h_[:, 1:NB + 1], ALU.mult, ALU.add, [xx_.b, vecs.b, h_.b], [xa_.b])
                                mm(p_la[0:96, 0:NB], a1t[:, kc, :], xa_[:], kc == 0, kc == KC - 1, [a1t.b, xa_.b], [p_la.b])
                                if e == 1:
                                    stt(xg_[:], xx_[:], V("mu5", kc), h_[:, 1:NB + 1], ALU.mult, ALU.add, [xx_.b, vecs.b, h_.b], [xg_.b])
                                    for j2 in range(2):
                                        mm(p_lg[j2][:, 0:NB], g1t[j2][:, kc, :], xg_[:], kc == 0, kc == KC - 1, [g1t[j2].b, xg_.b], [p_lg[j2].b])
                            act(Lw[:], p_lw[0:96, 0:NB], AF.Tanh, [p_lw.b], [Lw.b])
                            cp(ACT, La[:], p_la[0:96, 0:NB], [p_la.b], [La.b])
                            if e == 1:
                                for j2 in range(2):
                                    act(Lg[:, j2, :], p_lg[j2][:, 0:NB], AF.Sigmoid, [p_lg[j2].b], [Lg.b])
                            for pg in range(4):
                                cols = slice(pg * 512, (pg + 1) * 512)
                                dma(POOL, w2t[:], rw_w2[e][:, cols], [], [w2t.b])
                                dma(POOL, a2t[:], rw_a2[e][:, cols], [], [a2t.b])
                                if e == 1:
                                    dma(POOL, g2t[:], rw_g2.rearrange("(c p) n -> p c n", p=128)[:, :, cols], [], [g2t.b])
                                    dma(SP, o_fm[:], osc[:, pg * 4:pg * 4 + 4, g0:g0 + NB], [osc_b], [o_fm.b])
                                    dma(SP, bon[:], bsc[:, pg * 4:pg * 4 + 4, g0:g0 + NB], [bsc_b], [bon.b])
                                def prep(oc, Wr, Wk, Wv, pt):
                                    kcg = pg * 4 + oc
                                    osl = slice(0, 128)
                                    osl5 = slice(oc * 128, (oc + 1) * 128)
                                    p_r, p_k, p_v = pnext(), pnext(), pnext()
                                    for kc in range(KC):
                                        mm(p_r[:, 0:NB], Wr[:, kc, osl], Xr[:, kc, :], kc == 0, kc == KC - 1, [Wr.b, Xr.b], [p_r.b])
                                    for kc in range(KC):
                                        mm(p_k[:, 0:NB], Wk[:, kc, osl], Xk[:, kc, :], kc == 0, kc == KC - 1, [Wk.b, Xk.b], [p_k.b])
                                    p_w, p_a = pnext(), pnext()
                                    mm(p_w[:, 0:NB], w2t[:, osl5], Lw[:], True, True, [w2t.b, Lw.b], [p_w.b])
                                    mm(p_a[:, 0:NB], a2t[:, osl5], La[:], True, True, [a2t.b, La.b], [p_a.b])
                                    for kc in range(KC):
                                        mm(p_v[:, 0:NB], Wv[:, kc, osl], Xv[:, kc, :], kc == 0, kc == KC - 1, [Wv.b, Xv.b], [p_v.b])
                                    cp(ACT, pt["kf"][:], p_k[:, 0:NB], [p_k.b], [pt["kf"].b])
                                    act(pt["sg"][:], p_w[:, 0:NB], AF.Sigmoid, [p_w.b, vecs.b], [pt["sg"].b], bias=V(f"w0_{e}", kcg))
                                    act(pt["af"][:], p_a[:, 0:NB], AF.Sigmoid, [p_a.b, vecs.b], [pt["af"].b], bias=V(f"a0_{e}", kcg))
                                    cp(ACT, pt["rf"][:], p_r[:, 0:NB], [p_r.b], [pt["rf"].b])
                                    cp(ACT, pt["vf"][:], p_v[:, 0:NB], [p_v.b], [pt["vf"].b])
                                    yield
                                    ts(DVE, pt["u1"][:], pt["kf"][:], V("k_k", kcg), ALU.mult, [pt["kf"].b, vecs.b], [pt["u1"].b])
                                    tt(DVE, pt["u2"][:], pt["u1"][:], pt["u1"][:], ALU.mult, [pt["u1"].b], [pt["u2"].b])
                                    pb = pnext()
                                    mm(pb[:, 0:NB], bones[:], pt["u2"][:], True, True, [bones.b, pt["u2"].b], [pb.b])
                                    P.op(DVE, lambda e_: e_.tensor_tensor_scan(out=pt["cum"][:], data0=resetm[:], data1=pt["sg"][:], initial=0.0,
                                                                                  op0=ALU.mult, op1=ALU.add),
                                         reads=[resetm.b, pt["sg"].b], writes=[pt["cum"].b])
                                    yield
                                    cum3 = pt["cum"][:].rearrange("p (c t) -> p c t", t=64)
                                    if e == 1:
                                        tt(DVE, pt["Ep"][:].rearrange("p (c t) -> p c t", t=64), cum3[:, :, 63:64].to_broadcast([128, 4, 64]), cum3, ALU.subtract,
                                           [pt["cum"].b], [pt["Ep"].b])
                                        tt(DVE, pt["cum"][:], pt["Ep"][:], pt["sg"][:], ALU.add, [pt["Ep"].b, pt["sg"].b], [pt["cum"].b])
                                        totv = cum3[:, :, 0]
                                    else:
                                        totv = cum3[:, :, 63]
                                    tt(DVE, pt["Eex"][:], pt["cum"][:], pt["sg"][:], ALU.subtract, [pt["cum"].b, pt["sg"].b], [pt["Eex"].b])
                                    act(gC[:, oc, :], totv, AF.Exp, [pt["cum"].b], [gC.b], scale=-CD)
                                    act(pt["Ep"][:], pt["cum"][:], AF.Exp, [pt["cum"].b], [pt["Ep"].b], scale=-CD)
                                    act(pt["Em"][:], pt["cum"][:], AF.Exp, [pt["cum"].b], [pt["Em"].b], scale=CD)
                                    act(pt["Eex"][:], pt["Eex"][:], AF.Exp, [pt["Eex"].b], [pt["Eex"].b], scale=-CD)
                                    ts(DVE, pt["u2"][:], pt["af"][:], V("k_a", kcg), ALU.mult, [pt["af"].b, vecs.b, omka.b], [pt["u2"].b],
                                       s2=omka[:, kcg:kcg + 1], op1=ALU.add)
                                    tt(DVE, pt["kd"][:], pt["kf"][:], pt["u2"][:], ALU.mult, [pt["kf"].b, pt["u2"].b], [pt["kd"].b])
                                    yield
                                    act(pt["u3"][:], pb[:, 0:NB], AF.Sqrt, [pb.b], [pt["u3"].b])
                                    stt(pt["u2"][:], pt["rf"][:], V("r_k", kcg), pt["kd"][:], ALU.mult, ALU.mult, [pt["rf"].b, vecs.b, pt["kd"].b], [pt["u2"].b])
                                    p_b = pnext()
                                    mm(p_b[:, 0:NB], bones[:], pt["u2"][:], True, True, [bones.b, pt["u2"].b], [p_b.b])
                                    ts(DVE, pt["u3"][:], pt["u3"][:], 1e-12, ALU.max, [pt["u3"].b], [pt["u3"].b])
                                    P.op(DVE, lambda e_: e_.reciprocal(out=pt["u3"][:], in_=pt["u3"][:]), reads=[pt["u3"].b], writes=[pt["u3"].b])
                                    tt(DVE, pt["kk"][:], pt["u1"][:], pt["u3"][:], ALU.mult, [pt["u1"].b, pt["u3"].b], [pt["kk"].b])
                                    yield
                                    if e == 0:
                                        tt(DVE, bon[:, oc, :], p_b[:, 0:NB], pt["vf"][:], ALU.mult, [p_b.b, pt["vf"].b], [bon.b])
                                    else:
                                        tt(DVE, pt["u2"][:], p_b[:, 0:NB], pt["vf"][:], ALU.mult, [p_b.b, pt["vf"].b], [pt["u2"].b])
                                        tt(DVE, bon[:, oc, :], bon[:, oc, :], pt["u2"][:], ALU.add, [bon.b, pt["u2"].b], [bon.b])
                                    tt(DVE, pt["u1"][:], pt["kk"][:], pt["af"][:], ALU.mult, [pt["kk"].b, pt["af"].b], [pt["u1"].b])
                                    yield
                                    for hh in range(2):
                                        hs_ = half(hh)
                                        v3 = lambda t_: t_[hs_, :].rearrange("p (c t) -> p c t", t=64)
                                        bdv = lambda n_: BD[n_][hs_, :, oc, hh * 64:(hh + 1) * 64]
                                        stt(bdv("aT"), v3(pt["kk"]), -1.0, v3(pt["Eex"]), ALU.mult, ALU.mult, [pt["kk"].b, pt["Eex"].b], [BD["aT"].b])
                                        tt(DVE, bdv("bT"), v3(pt["u1"]), v3(pt["Em"]), ALU.mult, [pt["u1"].b, pt["Em"].b], [BD["bT"].b])
                                        tt(DVE, bdv("kT"), v3(pt["kd"]), v3(pt["Em"]), ALU.mult, [pt["kd"].b, pt["Em"].b], [BD["kT"].b])
                                        tt(DVE, bdv("rT"), v3(pt["rf"]), v3(pt["Ep"]), ALU.mult, [pt["rf"].b, pt["Ep"].b], [BD["rT"].b])
                                        cp(ACT, bdv("vT"), v3(pt["vf"]), [pt["vf"].b], [BD["vT"].b])
                                    yield

                                for o2 in range(2):
                                    gens_ = []
                                    for j_ in range(2):
                                        oc_ = o2 * 2 + j_
                                        c2 = slice(pg * 512 + oc_ * 128, pg * 512 + oc_ * 128 + 128)
                                        Wr, Wk, Wv = wbnext(), wbnext(), wbnext()
                                        dma(POOL, Wr[:], wr_v[:, :, c2], [], [Wr.b])
                                        dma(POOL, Wk[:], wk_v[:, :, c2], [], [Wk.b])
                                        dma(POOL, Wv[:], wv_v[:, :, c2], [], [Wv.b])
                                        gens_.append(prep(oc_, Wr, Wk, Wv, pts[j_]))
                                    interleave(gens_)
                                corder = list(range(4)) if e == 0 else list(range(3, -1, -1))
                                bd4 = lambda n_, c_: (BD[n_][:, c_], BD[n_].b)

                                def cst(c_, n_):
                                    t_ = CS[c_ % NSET][alias.get(n_, n_)]
                                    return (t_, t_.b)
                                idb = (identb, identb.b)
                                def par(c_):
                                    for src, dst in (("aT", "A_tm"), ("bT", "B_tm"), ("kT", "K_tm"), ("vT", "V_tm")):
                                        pb = pnext()
                                        mm4(pb, bd4(src, c_), idb)
                                        d_ = cst(c_, dst)
                                        cp(ACT, d_[0][:], pv(pb), [pb.b], [d_[1]])
                                    for (l_, r_, dst, msk) in (("bT", "aT", "P0", mBef), ("aT", "bT", "Q0", mBefT), ("kT", "aT", "N_ak", mBef),
                                                               ("bT", "rT", "N_rb", mInc), ("kT", "rT", "N_rk", mInc)):
                                        pb = pnext()
                                        mm4(pb, bd4(l_, c_), bd4(r_, c_))
                                        d_ = cst(c_, dst)
                                        tt(DVE, d_[0][:], pv(pb), msk[:], ALU.mult, [pb.b, msk.b], [d_[1]])
                                    yield
                                    X_ = [cst(c_, "Xb"), cst(c_, "Xc")]
                                    P0_ = cst(c_, "P0")
                                    tt(DVE, X_[0][0][:], P0_[0][:], identb[:], ALU.add, [P0_[1], identb.b], [X_[0][1]])
                                    yield
                                    for lev in range(5):
                                        pi, po_ = lev % 2, (lev + 1) % 2
                                        Pi, Qi = cst(c_, f"P{pi}"), cst(c_, f"Q{pi}")
                                        Po, Qo = cst(c_, f"P{po_}"), cst(c_, f"Q{po_}")
                                        pb = pnext()
                                        mm4(pb, Pi, Qi)
                                        cp(DVE if lev % 2 == 0 else ACT, Qo[0][:], pv(pb), [pb.b], [Qo[1]])
                                        if lev < 4:
                                            pb = pnext()
                                            mm4(pb, Qi, Pi)
                                            cp(ACT, Po[0][:], pv(pb), [pb.b], [Po[1]])
                                        yield
                                        Xi, Xo = X_[lev % 2], X_[(lev + 1) % 2]
                                        pb = pnext()
                                        mm4multi(pb, [(Qo, Xi, None, None), (idb, Xi, None, None)])
                                        cp(ACT, Xo[0][:], pv(pb), [pb.b], [Xo[1]])
                                        yield
                                    TT_ = X_[1]
                                    pb = pnext()
                                    mm4(pb, cst(c_, "N_ak"), cst(c_, "V_tm"))
                                    cp(ACT, cst(c_, "X1")[0][:], pv(pb), [pb.b], [cst(c_, "X1")[1]])
                                    pb = pnext()
                                    mm4(pb, TT_, cst(c_, "A_tm"))
                                    cp(DVE, cst(c_, "Ah")[0][:], pv(pb), [pb.b], [cst(c_, "Ah")[1]])
                                    yield
                                    pb = pnext()
                                    mm4(pb, TT_, cst(c_, "X1"))
                                    cp(ACT, cst(c_, "Ul")[0][:], pv(pb), [pb.b], [cst(c_, "Ul")[1]])
                                    pb = pnext()
                                    mm4(pb, cst(c_, "Ah"), cst(c_, "N_rb"))
                                    tt(DVE, cst(c_, "RhT")[0][:], pv(pb), BD["rT"][:, c_], ALU.add, [pb.b, BD["rT"].b], [cst(c_, "RhT")[1]])
                                    pb = pnext()
                                    mm4(pb, cst(c_, "Ah"), cst(c_, "B_tm"))
                                    tt(DVE, cst(c_, "GpT")[0][:], pv(pb), identb[:], ALU.add, [pb.b, identb.b], [cst(c_, "GpT")[1]])
                                    yield

                                def seq(c_):
                                    gcb = gC[:, :, c_:c_ + 1].to_broadcast([128, 4, 128])
                                    Mq = [Mb[:, pg * 4 + qi, :] for qi in range(4)]
                                    pb = pnext()
                                    mm4multi(pb, [(cst(c_, "Ul"), cst(c_, "N_rb"), None, None), (cst(c_, "V_tm"), cst(c_, "N_rk"), None, None),
                                                  ((None, Mb.b), cst(c_, "RhT"), Mq, None)])
                                    for hh in range(2):
                                        hs_ = half(hh)
                                        o_dst = o_fm[hs_, :, c_ * 64:(c_ + 1) * 64]
                                        o_src = pv(pb)[hs_, :, hh * 64:(hh + 1) * 64]
                                        if e == 0:
                                            cp(DVE, o_dst, o_src, [pb.b], [o_fm.b])
                                        else:
                                            tt(DVE, o_dst, o_src, o_dst, ALU.add, [pb.b, o_fm.b], [o_fm.b])
                                    pb = pnext()
                                    mm4multi(pb, [(cst(c_, "B_tm"), cst(c_, "Ul"), None, None), (cst(c_, "K_tm"), cst(c_, "V_tm"), None, None),
                                                  (cst(c_, "GpT"), (None, Mb.b), None, Mq)])
                                    Mf4 = Mf[:, pg * 4:pg * 4 + 4, :]
                                    tt(DVE, Mf4, pv(pb), gcb, ALU.mult, [pb.b, gC.b], [Mf.b])
                                    cp(ACT, Mb[:, pg * 4:pg * 4 + 4, :], Mf4, [Mf.b], [Mb.b])
                                for i_ in range(0, 4, NSET):
                                    interleave([par(c_) for c_ in corder[i_:i_ + NSET]])
                                    for c_ in corder[i_:i_ + NSET]:
                                        seq(c_)
                                if e == 0:
                                    dma(SP, osc[:, pg * 4:pg * 4 + 4, g0:g0 + NB], o_fm[:], [o_fm.b], [osc_b], chan=o_fm.b)
                                    dma(SP, bsc[:, pg * 4:pg * 4 + 4, g0:g0 + NB], bon[:], [bon.b], [bsc_b], chan=bon.b)
                                else:
                                    if _dbg:
                                        dma(SP, osc[:, pg * 4:pg * 4 + 4, g0:g0 + NB], o_fm[:], [o_fm.b], [osc_b], chan=o_fm.b)
                                        dma(SP, bsc[:, pg * 4:pg * 4 + 4, g0:g0 + NB], bon[:], [bon.b], [bsc_b], chan=bon.b)
                                    def gn(oc, pt):
                                        kcg = pg * 4 + oc
                                        osl5 = slice(oc * 128, (oc + 1) * 128)
                                        pb = pnext()
                                        mm(pb[:, 0:NB], bones[:], o_fm[:, oc, :], True, True, [bones.b, o_fm.b], [pb.b])
                                        pg_ = pnext()
                                        for j2 in range(2):
                                            mm(pg_[:, 0:NB], g2t[:, j2, osl5], Lg[:, j2, :], j2 == 0, j2 == 1, [g2t.b, Lg.b], [pg_.b])
                                        stt(pt["u1"][:], pb[:, 0:NB], -1.0 / 64, o_fm[:, oc, :], ALU.mult, ALU.add, [pb.b, o_fm.b], [pt["u1"].b])
                                        tt(DVE, pt["u2"][:], pt["u1"][:], pt["u1"][:], ALU.mult, [pt["u1"].b], [pt["u2"].b])
                                        yield
                                        pb = pnext()
                                        mm(pb[:, 0:NB], bones[:], pt["u2"][:], True, True, [bones.b, pt["u2"].b], [pb.b])
                                        act(pt["u3"][:], pb[:, 0:NB], AF.Sqrt, [pb.b, gneps.b], [pt["u3"].b], scale=1.0 / 64, bias=gneps[:, 0:1])
                                        yield
                                        P.op(DVE, lambda e_: e_.reciprocal(out=pt["u3"][:], in_=pt["u3"][:]), reads=[pt["u3"].b], writes=[pt["u3"].b])
                                        stt(pt["u1"][:], pt["u1"][:], V("ln_w", kcg), pt["u3"][:], ALU.mult, ALU.mult, [pt["u1"].b, vecs.b, pt["u3"].b], [pt["u1"].b])
                                        stt(pt["u1"][:], pt["u1"][:], V("ln_b", kcg), bon[:, oc, :], ALU.add, ALU.add, [pt["u1"].b, vecs.b, bon.b], [pt["u1"].b])
                                        tt(DVE, yb[:, kcg, :], pg_[:, 0:NB], pt["u1"][:], ALU.mult, [pg_.b, pt["u1"].b], [yb.b])
                                        yield
                                    for o2 in range(2):
                                        interleave([gn(o2 * 2 + j_, pts[j_]) for j_ in range(2)])
                            if e == 1:
                                for dc in range(KC):
                                    Wo = wbnext()
                                    dma(POOL, Wo[:], wo_v[:, :, dc * 128:(dc + 1) * 128], [], [Wo.b])
                                    for dd in range(1):
                                        xr_ = xres[dc % 2]
                                        dma(SP, xr_[:], scr_v[:, dc, g0:g0 + NB], scr_b, [xr_.b])
                                        pb = pnext()
                                        for kc in range(KC):
                                            mm(pb[:, 0:NB], Wo[:, kc, :], yb[:, kc, :], kc == 0, kc == KC - 1, [Wo.b, yb.b], [pb.b])
                                        stt(xr_[:], pb[:, 0:NB], mod[:, 32 + dc, cc:cc + 1], xr_[:], ALU.mult, ALU.add, [pb.b, mod.b, xr_.b], [xr_.b])
                                        dma(SP, scr_v2[1 - scur][:, dc, g0:g0 + NB], xr_[:], [xr_.b], scr_b2[1 - scur], chan=xr_.b)
                            if last_of_seq and cc == 1:
                                pr = (q0 - 2048) // 256
                                for p_ in range(KC):
                                    pb = pnext()
                                    tr(pb[:, 0:128], Mf[:, p_, :], [Mf.b], [pb.b])
                                    so_ = so[p_ % 2]
                                    cp(ACT, so_[:], pb[:, 0:128], [pb.b], [so_.b])
                                    for hh in range(2):
                                        dma(SP, st_out[pr, e, p_ * 128 + hh * 64:p_ * 128 + hh * 64 + 64, :], so_[half(hh), hh * 64:(hh + 1) * 64],
                                            [so_.b], [st_b], chan=so_.b)
                scur = 1 - scur
                scr_v, scr_b = scr_v2[scur], scr_b2[scur]
                P.barrier()

            if mixers and kind == 2:
                with ExitStack() as ph:
                    kT_all = sb("kT_all", [128, 4, NTOK], BF16, stack=ph)
                    V_all = sb("V_all", [128, 20, 512], BF16, stack=ph)
                    kTc = sb("kTc", [128, 4, 512], BF16, stack=ph)
                    V_c = sb("V_c", [128, 4, 512], BF16, stack=ph)
                    masks = sb("amasks", [128, 6, 512], BF16, stack=ph)
                    rotT = sb("rotT", [128, 128], F32, stack=ph)
                    esink = sb("esink", [128, 16], F32, stack=ph)
                    onesb = sb("onesb", [128, 128], BF16, stack=ph)
                    xs = sb("axs", [128, KC, 512], F32, stack=ph)
                    hT = sb("ahT", [128, KC, 512], BF16, stack=ph)
                    rr = sb("arr", [128, 512], F32, stack=ph)
                    sqt = [sb(f"asq{i}", [128, 512], F32, stack=ph) for i in range(2)]
                    tmpn = [sb(f"atmp{i}", [128, 512], F32, stack=ph) for i in range(2)]
                    ropeC = sb("ropeC", [128, 512], F32, stack=ph)
                    ropeS = sb("ropeS", [128, 512], F32, stack=ph)
                    qsq = sb("qsq", [128, 512], F32, stack=ph)
                    qr = sb("qr", [128, 512], F32, stack=ph)
                    qn = sb("qn", [128, 512], F32, stack=ph)
                    t1 = sb("at1", [128, 512], F32, stack=ph)
                    t2 = sb("at2", [128, 512], F32, stack=ph)
                    epsh = epsb
                    ck_b, cv_b = Buf("ck_out"), Buf("cv_out")

                    dma(SP, masks[:], amask_d, [], [masks.b])
                    dma(SP, rotT[:], rotT_d, [], [rotT.b])
                    memset(DVE, onesb[:], 1.0, [onesb.b])
                    act(esink[:], V("sink"), AF.Exp, [vecs.b], [esink.b])
                    dma(POOL, V_c[:], cache_v.rearrange("(t p) n -> p t n", p=128), [], [V_c.b])

                    def prenorm_seg(si):
                        s0, sl, cc = SEGS[si]
                        dma(SP, xs[:], scr_v[:, :, s0:s0 + sl], [scr_b[si]], [xs.b])
                        rstd_cols(lambda kc: xs[:, kc, :], 512, rr[:], rr.b, xs.b, sqt, psum[1])
                        for kc in range(KC):
                            tm = tmpn[kc % 2]
                            stt(tm[:], xs[:, kc, :], gs1[:, kc, cc:cc + 1], rr[:], ALU.mult, ALU.mult, [xs.b, gs1.b, rr.b], [tm.b])
                            act(hT[:, kc, :], tm[:], AF.Identity, [tm.b, mod.b], [hT.b], bias=mod[:, kc, cc:cc + 1])
                        if cc == 0:
                            dma(SP, ropeC[:], ropeC_d[:, s0:s0 + 512], [], [ropeC.b])
                            dma(SP, ropeS[:], ropeS_d[:, s0:s0 + 512], [], [ropeS.b])

                    def qk_head(pb, gname, rope, out_ap, out_b):
                        act(qsq[:], pb[:, 0:512], AF.Square, [pb.b], [qsq.b])
                        mm(psum[1][:, 0:512], ones[:], qsq[:], True, True, [ones.b, qsq.b], [psum[1].b])
                        act(qr[:], psum[1][:, 0:512], AF.Sqrt, [psum[1].b, epsb.b], [qr.b], scale=1.0 / 128, bias=epsb[:, 0:1])
                        P.op(DVE, lambda e: e.reciprocal(out=qr[:], in_=qr[:]), reads=[qr.b], writes=[qr.b])
                        stt(qn[:], pb[:, 0:512], V(gname), qr[:], ALU.mult, ALU.mult, [pb.b, vecs.b, qr.b], [qn.b])
                        if rope:
                            mm(psum[2][:, 0:512], rotT[:], qn[:], True, True, [rotT.b, qn.b], [psum[2].b])
                            tt(DVE, t1[:], qn[:], ropeC[:], ALU.mult, [qn.b, ropeC.b], [t1.b])
                            tt(DVE, t2[:], psum[2][:, 0:512], ropeS[:], ALU.mult, [psum[2].b, ropeS.b], [t2.b])
                            tt(DVE, out_ap, t1[:], t2[:], ALU.add, [t1.b, t2.b], [out_b])
                        else:
                            cp(DVE, out_ap, qn[:], [qn.b], [out_b])

                    with ExitStack() as phA:
                        wk = sb("awk", [128, KC, 512], BF16, stack=phA)
                        wv_ = sb("awv", [128, KC, 512], BF16, stack=phA)
                        ckt = sb("ckt", [128, 4, 512], F32, stack=phA)
                        kout = [sb(f"kout{i}", [128, 512], F32, stack=phA) for i in range(2)]
                        vout = [sb(f"vout{i}", [128, 512], F32, stack=phA) for i in range(2)]
                        qkv_v = w_qkv.rearrange("(c p) n -> p c n", p=128)
                        dma(POOL, wk[:], qkv_v[:, :, 2048:2560], [], [wk.b])
                        dma(POOL, wv_[:], qkv_v[:, :, 2560:3072], [], [wv_.b])
                        dma(SP, ckt[:], cache_k.rearrange("(t p) n -> p t n", p=128), [], [ckt.b])
                        import os as _os
                        ASTG = int(_os.environ.get("ATT_STAGE", "9"))
                        for kh in range(4 if ASTG >= 2 else 0):
                            pb = psum[4 + kh]
                            for t4 in range(4):
                                tr(pb[:, t4 * 128:(t4 + 1) * 128], ckt[:, t4, kh * 128:(kh + 1) * 128], [ckt.b], [pb.b])
                            cp(ACT, kTc[:, kh, :], pb[:, 0:512], [pb.b], [kTc.b])
                        for si, (s0, sl, cc) in enumerate(SEGS[:(0 if ASTG < 3 else (4 if ASTG == 3 else 5))]):
                            prenorm_seg(si)
                            for kh in range(4):
                                pb = psum[0]
                                for kc in range(KC):
                                    mm(pb[:, 0:512], wk[:, kc, kh * 128:(kh + 1) * 128], hT[:, kc, :], kc == 0, kc == KC - 1, [wk.b, hT.b], [pb.b])
                                qk_head(pb, "kn", cc == 0, kT_all[:, kh, s0:s0 + 512], kT_all.b)
                                if cc == 1 and ASTG != 5:
                                    for t4 in range(4):
                                        tr(psum[4 + t4][:, kh * 128:(kh + 1) * 128], qn[:, t4 * 128:(t4 + 1) * 128], [qn.b], [psum[4 + t4].b])
                            if cc == 1 and ASTG != 5:
                                for t4 in range(4):
                                    ko = kout[t4 % 2]
                                    cp(ACT, ko[:], psum[4 + t4][:, 0:512], [psum[4 + t4].b], [ko.b])
                                    dma(SP, ck_out[t4 // 2, (t4 % 2) * 128:(t4 % 2) * 128 + 128, :], ko[:], [ko.b], [ck_b], chan=ko.b)
                            for t4 in range(4):
                                t_ = (s0 // 128) + t4
                                pb = psum[3]
                                for kc in range(KC):
                                    mm(pb[:, 0:512], hT[:, kc, t4 * 128:(t4 + 1) * 128], wv_[:, kc, :], kc == 0, kc == KC - 1, [hT.b, wv_.b], [pb.b])
                                cp(ACT, V_all[:, t_, :], pb[:, 0:512], [pb.b], [V_all.b])
                                if cc == 1 and ASTG != 6:
                                    vo = vout[t4 % 2]
                                    cp(DVE, vo[:], pb[:, 0:512], [pb.b], [vo.b])
                                    dma(SP, cv_out[t4 // 2, (t4 % 2) * 128:(t4 % 2) * 128 + 128, :], vo[:], [vo.b], [cv_b], chan=vo.b)
                    P.barrier()

                    with ExitStack() as phB:
                        wts = [sb(f"abw{i}", [128, KC, 512], BF16, stack=phB) for i in range(2)]
                        oT = sb("aoT", [128, 16, 512], BF16, stack=phB)
                        qT = [sb(f"aqT{i}", [128, 512], BF16, stack=phB) for i in range(2)]
                        Et = [sb(f"aE{i}", [128, 512], BF16, stack=phB) for i in range(3)]
                        rden = sb("rden", [128, 512], F32, stack=phB)
                        qkv_v = w_qkv.rearrange("(c p) n -> p c n", p=128)
                        wo_v = w_ao.rearrange("(c p) n -> p c n", p=128)
                        wc = [0]
                        ec = [0]

                        def wnext():
                            w = wts[wc[0] % 2]
                            wc[0] += 1
                            return w
                        import os as _os
                        for si, (s0, sl, cc) in enumerate(SEGS if _os.environ.get('ATT_PASSB', '1') == '1' else []):
                            prenorm_seg(si)
                            qw = [None]

                            def qproj(h):
                                if h % 4 == 0:
                                    qw[0] = wnext()
                                    dma(POOL, qw[0][:], qkv_v[:, :, (h // 4) * 512:(h // 4 + 1) * 512], [], [qw[0].b])
                                w = qw[0]
                                pb = psum[0]
                                for kc in range(KC):
                                    mm(pb[:, 0:512], w[:, kc, (h % 4) * 128:(h % 4 + 1) * 128], hT[:, kc, :], kc == 0, kc == KC - 1, [w.b, hT.b], [pb.b])
                                q = qT[h % 2]
                                qk_head(pb, "qn", cc == 0, q[:], q.b)

                            def attend(h):
                                kv = h // 4
                                q = qT[h % 2]
                                pden, po = psum[5], psum[6]
                                if cc == 0:
                                    groups = [(0, 512, [("l", kb) for kb in range(max(0, s0 // 128 - 1), min(15, s0 // 128 + 4) + 1)] + [("c", cb) for cb in range(4)])]
                                else:
                                    groups = [(pr * 256, 256, [("l", 16 + 2 * pr), ("l", 17 + 2 * pr)]) for pr in range(2)]
                                for (c0, n, blocks) in groups:
                                    def operands(typ, kb):
                                        if typ == "l":
                                            return (kT_all[:, kv, kb * 128:(kb + 1) * 128], kT_all.b, V_all[:, kb, kv * 128:(kv + 1) * 128], V_all.b)
                                        return (kTc[:, kv, kb * 128:(kb + 1) * 128], kTc.b, V_c[:, kb, kv * 128:(kv + 1) * 128], V_c.b)

                                    def scores(bi):
                                        typ, kb = blocks[bi]
                                        kk_ap, kk_b, _, _ = operands(typ, kb)
                                        pS = psum[3 + (ec[0] + bi) % 2]
                                        mm(pS[:, 0:n], kk_ap, q[:, c0:c0 + n], True, True, [kk_b, q.b], [pS.b])
                                    scores(0)
                                    for bi, (typ, kb) in enumerate(blocks):
                                        if bi + 1 < len(blocks):
                                            scores(bi + 1)
                                        pS = psum[3 + (ec[0] + bi) % 2]
                                        E = Et[(ec[0] + bi) % 3]
                                        _, _, vv_ap, vv_b = operands(typ, kb)
                                        act(E[:, 0:n], pS[:, 0:n], AF.Exp, [pS.b], [E.b], scale=float(128 ** -0.5))
                                        if typ == "l" and cc == 0:
                                            r = kb - s0 // 128 + 1
                                            tt(DVE, E[:, 0:n], E[:, 0:n], masks[:, r, 0:n], ALU.mult, [E.b, masks.b], [E.b])
                                        first, last = bi == 0, bi == len(blocks) - 1
                                        mm(pden[:, c0:c0 + n], onesb[:], E[:, 0:n], first, last, [onesb.b, E.b], [pden.b])
                                        mm(po[:, c0:c0 + n], vv_ap, E[:, 0:n], first, last, [vv_b, E.b], [po.b])
                                    ec[0] += len(blocks)
                                ts(DVE, rden[:], pden[:, 0:512], esink[:, h:h + 1], ALU.add, [pden.b, esink.b], [rden.b])
                                P.op(DVE, lambda e: e.reciprocal(out=rden[:], in_=rden[:]), reads=[rden.b], writes=[rden.b])
                                tt(DVE, oT[:, h, :], po[:, 0:512], rden[:], ALU.mult, [po.b, rden.b], [oT.b])

                            qproj(0)
                            for h in range(16):
                                if h + 1 < 16:
                                    qproj(h + 1)
                                attend(h)
                            for d4 in range(4):
                                w = wnext()
                                dma(POOL, w[:], wo_v[:, :, d4 * 512:(d4 + 1) * 512], [], [w.b])
                                for dd in range(4):
                                    dc = d4 * 4 + dd
                                    pb = psum[7] if dd % 2 == 0 else psum[0]
                                    for h in range(16):
                                        mm(pb[:, 0:512], w[:, h, dd * 128:(dd + 1) * 128], oT[:, h, :], h == 0, h == 15, [w.b, oT.b], [pb.b])
                                    stt(xs[:, dc, :], pb[:, 0:512], mod[:, 32 + dc, cc:cc + 1], xs[:, dc, :], ALU.mult, ALU.add, [pb.b, mod.b, xs.b], [xs.b])
                            dma(SP, scr_v2[1 - scur][:, :, s0:s0 + 512], xs[:], [xs.b], [scr_b2[1 - scur][si]], chan=xs.b)
                scur = 1 - scur
                scr_v, scr_b = scr_v2[scur], scr_b2[scur]
                P.barrier()

            if ffn:
                with ExitStack() as ph:
                    XW = 520
                    xs = sb("fxs", [128, KC, XW], F32, stack=ph)
                    hT = sb("fhT", [128, KC, XW], BF16, stack=ph)
                    actT = sb("actT", [128, FC, 512], BF16, stack=ph)
                    rr = sb("frr", [128, XW], F32, stack=ph)
                    sqt = [sb(f"fsq{i}", [128, 512], F32, stack=ph) for i in range(2)]
                    tmpn = [sb(f"ftmp{i}", [128, XW], F32, stack=ph) for i in range(2)]
                    stg = [[sb(f"stg{i}{j}", [128, XW], F32, stack=ph) for j in range(2)] for i in range(2)]
                    acc = [[sb(f"acc{i}{j}", [128, 512], F32, stack=ph) for j in range(2)] for i in range(2)]
                    sg = [sb(f"sg{i}", [128, 512], F32, stack=ph) for i in range(2)]
                    WSZ = 11264
                    NWB = 3
                    wts = [sb(f"fw{i}", [128, WSZ], BF16, stack=ph) for i in range(NWB)]
                    up_v = ffn_up[l].rearrange("(c p) (g f) -> p c g f", p=128, g=2)
                    dn_v = ffn_down[l].rearrange("(c p) n -> p c n", p=128)
                    for i in range(2):
                        for j in range(2):
                            memset(DVE, stg[i][j][:], 0.0, [stg[i][j].b])
                    wctr = [0]

                    def wnext():
                        w = wts[wctr[0] % NWB]
                        wctr[0] += 1
                        return w

                    for si, (s0, sl, cc) in enumerate(SEGS):
                        sample = cc == 0
                        if sample:
                            runs = [(1, 512)]
                            lo_t = max(s0 - 1, 0)
                            hi_t = min(s0 + 513, 2048)
                            c_lo = 1 - (s0 - lo_t)
                            if s0 == 0:
                                memset(DVE, xs[:, :, 0:1], 0.0, [xs.b])
                            if s0 + 512 == 2048:
                                memset(DVE, xs[:, :, 513:514], 0.0, [xs.b])
                            dma(SP, xs[:, :, c_lo:c_lo + (hi_t - lo_t)], scr_v[:, :, lo_t:hi_t], [scr_b[si]] + ([scr_b[si - 1]] if si > 0 else []) + ([scr_b[si + 1]] if si < 3 else []), [xs.b])
                            mainv = lambda t_, kc: t_[:, kc, 1:513]
                            halov = lambda t_, kc: t_[:, kc, 0:514:513]
                            main2 = lambda t_: t_[:, 1:513]
                            halo2 = lambda t_: t_[:, 0:514:513]
                        else:
                            runs = [(1, 256), (259, 256)]
                            for r_ in range(2):
                                dma(SP, xs[:, :, 1 + 258 * r_:257 + 258 * r_], scr_v[:, :, s0 + 256 * r_:s0 + 256 * r_ + 256], [scr_b[si]], [xs.b])
                            mainv = lambda t_, kc: t_[:, kc, 1:517].rearrange("p (r c) -> p r c", c=258)[:, :, 0:256]
                            main2 = lambda t_: t_[:, 1:517].rearrange("p (r c) -> p r c", c=258)[:, :, 0:256]
                            memset(DVE, hT[:], 0.0, [hT.b])
                            for i in range(2):
                                for j in range(2):
                                    memset(DVE, stg[i][j][:, 0:1], 0.0, [stg[i][j].b])
                                    memset(DVE, stg[i][j][:, 257:259], 0.0, [stg[i][j].b])
                                    memset(DVE, stg[i][j][:, 515:516], 0.0, [stg[i][j].b])
                        ps2 = lambda pb: pb[:, 0:512].rearrange("p (r c) -> p r c", c=256) if not sample else pb[:, 0:512]
                        pm, phb = psum[6], psum[7]
                        for kc in range(KC):
                            q = sqt[kc % 2]
                            q2 = q[:, 0:512].rearrange("p (r c) -> p r c", c=256) if not sample else q[:, 0:512]
                            act(q2, mainv(xs, kc), AF.Square, [xs.b], [q.b])
                            mm(pm[:, 0:512], ones[:], q[:, 0:512], kc == 0, kc == KC - 1, [ones.b, q.b], [pm.b])
                        act(main2(rr), ps2(pm), AF.Sqrt, [pm.b, epsb.b], [rr.b], scale=1.0 / D, bias=epsb[:, 0:1])
                        if sample:
                            for kc in range(KC):
                                q = sqt[kc % 2]
                                act(q[:, 0:2], halov(xs, kc), AF.Square, [xs.b], [q.b])
                                mm(phb[:, 0:2], ones[:], q[:, 0:2], kc == 0, kc == KC - 1, [ones.b, q.b], [phb.b])
                            act(halo2(rr), phb[:, 0:2], AF.Sqrt, [phb.b, epsb.b], [rr.b], scale=1.0 / D, bias=epsb[:, 0:1])
                            P.op(DVE, lambda e: e.reciprocal(out=rr[:, 0:514], in_=rr[:, 0:514]), reads=[rr.b], writes=[rr.b])
                        else:
                            P.op(DVE, lambda e: e.reciprocal(out=main2(rr), in_=main2(rr)), reads=[rr.b], writes=[rr.b])
                        for kc in range(KC):
                            tm = tmpn[kc % 2]
                            if sample:
                                stt(tm[:, 0:514], xs[:, kc, 0:514], gs2[:, kc, cc:cc + 1], rr[:, 0:514], ALU.mult, ALU.mult,
                                    [xs.b, gs2.b, rr.b], [tm.b])
                                act(hT[:, kc, 0:514], tm[:, 0:514], AF.Identity, [tm.b, mod.b], [hT.b], bias=mod[:, 48 + kc, cc:cc + 1])
                            else:
                                stt(main2(tm), mainv(xs, kc), gs2[:, kc, cc:cc + 1], main2(rr), ALU.mult, ALU.mult,
                                    [xs.b, gs2.b, rr.b], [tm.b])
                                act(mainv(hT, kc), main2(tm), AF.Identity, [tm.b, mod.b], [hT.b], bias=mod[:, 48 + kc, cc:cc + 1])
                        if sample:
                            if s0 == 0:
                                memset(DVE, hT[:, :, 0:1], 0.0, [hT.b])
                            if s0 + 512 == 2048:
                                memset(DVE, hT[:, :, 513:514], 0.0, [hT.b])
                        for st2 in range(FC // 2):
                            w = wnext()
                            wv = w[:, 0:KC * 2 * 256].rearrange("p (c g f) -> p c g f", c=KC, g=2)
                            for g_ in range(2):
                                dma(POOL, wv[:, :, g_, :], up_v[:, :, g_, st2 * 256:(st2 + 1) * 256], [], [w.b])
                            for f2 in range(2):
                                fc = st2 * 2 + f2
                                bset = fc % 2
                                pg, pv, phh = psum[bset * 3], psum[bset * 3 + 1], psum[bset * 3 + 2]
                                for g, pb in ((0, pg), (1, pv)):
                                    for kc in range(KC):
                                        mm(ps2(pb), wv[:, kc, g, f2 * 128:(f2 + 1) * 128], mainv(hT, kc), kc == 0, kc == KC - 1,
                                           [w.b, hT.b], [pb.b])
                                    if sample:
                                        for kc in range(KC):
                                            mm(phh[:, 2 * g:2 * g + 2], wv[:, kc, g, f2 * 128:(f2 + 1) * 128], halov(hT, kc), kc == 0, kc == KC - 1,
                                               [w.b, hT.b], [phh.b])
                                for g, pb in ((0, pg), (1, pv)):
                                    sgt = stg[bset][g]
                                    ag = acc[bset][g]
                                    cp(ACT, main2(sgt), ps2(pb), [pb.b], [sgt.b])
                                    if sample:
                                        cp(ACT, halo2(sgt), phh[:, 2 * g:2 * g + 2], [phh.b], [sgt.b])
                                    fcol = g * FC + fc
                                    for ri, (c0, n) in enumerate(runs):
                                        a_ = ag[:, ri * 256:ri * 256 + n]
                                        ts(DVE, a_, sgt[:, c0:c0 + n], V(f"cw{l}_1", fcol), ALU.mult, [sgt.b, vecs.b], [ag.b],
                                           s2=V(f"cb{l}", fcol), op1=ALU.add)
                                        stt(a_, sgt[:, c0 - 1:c0 - 1 + n], V(f"cw{l}_0", fcol), a_, ALU.mult, ALU.add, [sgt.b, vecs.b, ag.b], [ag.b])
                                        stt(a_, sgt[:, c0 + 1:c0 + 1 + n], V(f"cw{l}_2", fcol), a_, ALU.mult, ALU.add, [sgt.b, vecs.b, ag.b], [ag.b])
                                s_ = sg[bset]
                                act(s_[:], acc[bset][0][:], AF.Silu, [acc[bset][0].b], [s_.b])
                                tt(DVE, actT[:, fc, :], s_[:], acc[bset][1][:], ALU.mult, [s_.b, acc[bset][1].b], [actT.b])
                        for d2 in range(KC // 2):
                            w = wnext()
                            wv = w[:, 0:FC * 256].rearrange("p (c n) -> p c n", c=FC)
                            dma(POOL, wv, dn_v[:, :, d2 * 256:(d2 + 1) * 256], [], [w.b])
                            for dd in range(2):
                                dc = d2 * 2 + dd
                                pb = psum[dc % 4]
                                for fc in range(FC):
                                    mm(pb[:, 0:512], wv[:, fc, dd * 128:(dd + 1) * 128], actT[:, fc, :], fc == 0, fc == FC - 1,
                                       [w.b, actT.b], [pb.b])
                                stt(mainv(xs, dc), ps2(pb), mod[:, 80 + dc, cc:cc + 1], mainv(xs, dc), ALU.mult, ALU.add,
                                    [pb.b, mod.b, xs.b], [xs.b])
                        if sample:
                            dma(SP, scr_v2[1 - scur][:, :, s0:s0 + 512], xs[:, :, 1:513], [xs.b], [scr_b2[1 - scur][si]], chan=xs.b)
                        else:
                            for r_ in range(2):
                                dma(SP, scr_v2[1 - scur][:, :, s0 + 256 * r_:s0 + 256 * r_ + 256], xs[:, :, 1 + 258 * r_:257 + 258 * r_],
                                    [xs.b], [scr_b2[1 - scur][si]], chan=xs.b)
                scur = 1 - scur
                scr_v, scr_b = scr_v2[scur], scr_b2[scur]
                P.barrier()

        with ExitStack() as ph:
            xt = [sb(f"oxt{i}", [128, D], F32, stack=ph) for i in range(2)]
            xst = [sb(f"oxst{i}", [128, KC, 128], F32, stack=ph) for i in range(2)]
            for t_ in range(NTOK // 128):
                a, s_ = xt[t_ % 2], xst[t_ % 2]
                dma(SP, s_[:], scr_v[:, :, t_ * 128:(t_ + 1) * 128], [scr_b[t_ // 4]], [s_.b])
                for g in range(4):
                    pb = psum[(t_ * 4 + g) % 8]
                    for j in range(4):
                        kc = g * 4 + j
                        tr(pb[:, j * 128:(j + 1) * 128], s_[:, kc, :], [s_.b], [pb.b])
                    cp(ACT if g % 2 == 0 else DVE, a[:, g * 512:(g + 1) * 512], pb[:, 0:512], [pb.b], [a.b])
                dma(SP, y[t_ * 128:(t_ + 1) * 128, :], a[:], [a.b], [y_b], chan=a.b)
        P.barrier()
        P.emit()
    return nc


def attn_consts():
    import ml_dtypes
    i = np.arange(128)[:, None, None]
    r = np.arange(6)[None, :, None] - 1
    j = np.arange(512)[None, None, :]
    amask = (np.abs(j - (128 * r + i)) <= 128).astype(np.float32)
    rot = np.zeros((128, 128), np.float32)
    for a in range(2):
        for f in range(32):
            m0, m1 = a * 64 + f, a * 64 + 32 + f
            rot[m1, m0] = -1.0
            rot[m0, m1] = 1.0
    t = np.arange(2048)
    row = (t // 64).astype(np.float32)
    col = (t % 64).astype(np.float32)
    inv = (np.float32(10000.0) ** (-np.arange(32, dtype=np.float32) / np.float32(32))).astype(np.float32)
    ang = np.zeros((128, 2048), np.float32)
    for a, pos in enumerate((row, col)):
        for hlf in range(2):
            ang[a * 64 + hlf * 32:a * 64 + hlf * 32 + 32, :] = inv[:, None] * pos[None, :]
    return {"amask": amask.astype(ml_dtypes.bfloat16), "rotT": rot, "ropeC": np.cos(ang).astype(np.float32), "ropeS": np.sin(ang).astype(np.float32)}


def rwkv_consts():
    import ml_dtypes
    i = np.arange(64)[:, None]
    j = np.arange(64)[None, :]
    conds = [i < j, i > j, i <= j, i >= j]
    rmask = np.zeros((4, 128, 4, 128), np.float32)
    for m, c in enumerate(conds):
        for h in range(2):
            rmask[m, h * 64:(h + 1) * 64, :, h * 64:(h + 1) * 64] = c[:, None, :]
    identb = np.zeros((128, 4, 128), np.float32)
    identb[:] = np.eye(128, dtype=np.float32)[:, None, :]
    bones = np.zeros((128, 128), np.float32)
    for h in range(2):
        bones[h * 64:(h + 1) * 64, h * 64:(h + 1) * 64] = 1.0
    reset = np.ones((128, 256), np.float32)
    reset[:, ::64] = 0.0
    return {"rmask": rmask.astype(ml_dtypes.bfloat16), "identb": identb.astype(ml_dtypes.bfloat16), "bones": bones, "resetm": reset}


def core_inputs(inp, i, shared):
    xin = np.concatenate([inp["x_sample"][i], inp["x_prompt"][2 * i], inp["x_prompt"][2 * i + 1]], axis=0)
    condT = np.stack([fm(inp["c"][i]), fm(inp["c_ctx"])], axis=-1)
    m = {"xin": np.ascontiguousarray(xin, np.float32), "condT": np.ascontiguousarray(condT, np.float32),
         "cache_k": np.ascontiguousarray(inp["cache_k"][i, 0].reshape(512, 512), np.float32),
         "cache_v": np.ascontiguousarray(inp["cache_v"][i, 0].reshape(512, 512), np.float32),
         "st_in": np.ascontiguousarray(inp["state_rwkv"][i, 0].reshape(2, 2048, 64), np.float32)}
    m.update(shared)
    return m


def shared_inputs(inp):
    return {
        "vecs": build_vecs(inp),
        "ident": np.eye(128, dtype=np.float32),
        "ada_w": np.asarray(inp["ada_w"], np.float32),
        "ffn_up": np.asarray(inp["ffn_up"], np.float32),
        "ffn_down": np.asarray(inp["ffn_down"], np.float32),
        "pool_w": np.asarray(inp["pool_w"], np.float32),
        "w_qkv": np.asarray(inp["attn_w_qkv"][0], np.float32),
        "w_ao": np.asarray(inp["attn_w_o"][0], np.float32),
        **attn_consts(),
        **rwkv_consts(),
        "rw_r": np.asarray(inp["rwkv_w_r"][0], np.float32), "rw_k": np.asarray(inp["rwkv_w_k"][0], np.float32),
        "rw_v": np.asarray(inp["rwkv_w_v"][0], np.float32), "rw_o": np.asarray(inp["rwkv_w_o"][0], np.float32),
        "rw_w1": np.asarray(inp["rwkv_w1"][0], np.float32), "rw_w2": np.asarray(inp["rwkv_w2"][0], np.float32),
        "rw_a1": np.asarray(inp["rwkv_a1"][0], np.float32), "rw_a2": np.asarray(inp["rwkv_a2"][0], np.float32),
        "rw_g1": np.asarray(inp["rwkv_g1"][0], np.float32), "rw_g2": np.asarray(inp["rwkv_g2"][0], np.float32),
    }


def kernel(**inputs):
    inp = {k: np.asarray(v) for k, v in inputs.items()}
    nc = build()
    shared = shared_inputs(inp)
    in_maps = [core_inputs(inp, i, shared) for i in range(8)]
    res = run_bass_kernel_spmd(nc, in_maps, core_ids=list(range(8)))
    ys = [r["y"] for r in res.results]
    y_sample = np.stack([ys[i][0:2048] for i in range(8)], axis=0)
    y_prompt = np.stack([ys[i // 2][2048 + 256 * (i % 2):2048 + 256 * (i % 2) + 256] for i in range(16)], axis=0)
    new_state = np.stack([res.results[i // 2]["st_out"][i % 2].reshape(2, 32, 64, 64) for i in range(16)], axis=0)[:, None]
    new_ck = np.stack([res.results[i // 2]["ck_out"][i % 2].reshape(256, 4, 128) for i in range(16)], axis=0)[:, None]
    new_cv = np.stack([res.results[i // 2]["cv_out"][i % 2].reshape(256, 4, 128) for i in range(16)], axis=0)[:, None]
    f32 = lambda a: np.ascontiguousarray(a, dtype=np.float32)
    return f32(y_prompt), f32(y_sample), f32(new_state), f32(new_ck), f32(new_cv)
```
